# Optimizing a Trainium2 kernel written in Bass

```python
import jax, jax.numpy as jnp
from jax import lax
import numpy as np

D_MODEL = 1024
BATCH = 8
SEQ = 2048
DEPTH = 1

CTX_LEN = 256
GRID_W = 64
FOURIER_GROUPS = 4
FOURIER_GROUP_DIM = 128
FOURIER_WIDTH = FOURIER_GROUPS * FOURIER_GROUP_DIM
MLA_HEADS = 8
QK_NOPE_DIM = 64
QK_ROPE_DIM = 32
V_HEAD_DIM = 64
Q_LORA_RANK = 256
KV_LORA_RANK = 128
MIX_WIDTH = FOURIER_WIDTH + MLA_HEADS * V_HEAD_DIM
KV_COL = FOURIER_WIDTH + Q_LORA_RANK
IN_WIDTH = FOURIER_WIDTH + Q_LORA_RANK + KV_LORA_RANK + QK_ROPE_DIM
D_FF = -(-8 * D_MODEL // (3 * 256)) * 256
ROPE_BASE = 10000.0
NORM_EPS = 1e-6
Q_BLOCK = 128

kernel_name = "hybrid_fnet_mla_dit_block"


def rms_norm(x, g):
    x32 = x.astype(jnp.float32)
    y = x32 * lax.rsqrt(jnp.mean(x32 * x32, axis=-1, keepdims=True) + NORM_EPS)
    return (y * g.astype(jnp.float32)).astype(x.dtype)


def modulate(h, shift, scale):
    return h * (1.0 + scale) + shift


def _rotate(x, cos, sin):
    half = x.shape[-1] // 2
    x1, x2 = x[..., :half], x[..., half:]
    return jnp.concatenate([x1 * cos - x2 * sin, x2 * cos + x1 * sin], axis=-1)


def axial_rope(x, row, col):
    axis_dim = QK_ROPE_DIM // 2
    half = axis_dim // 2
    inv_freq = ROPE_BASE ** (-jnp.arange(half, dtype=jnp.float32) / half)
    shape = (row.shape[0],) + (1,) * (x.ndim - 3) + (half,)

    def tables(pos):
        ang = pos.astype(jnp.float32)[:, None] * inv_freq[None, :]
        return (jnp.cos(ang).reshape(shape).astype(x.dtype),
                jnp.sin(ang).reshape(shape).astype(x.dtype))

    xr, xc = x[..., :axis_dim], x[..., axis_dim:]
    return jnp.concatenate([_rotate(xr, *tables(row)), _rotate(xc, *tables(col))], axis=-1)


def mla_queries(q_a, g_q_a, w_q_b):
    B, T, _ = q_a.shape
    q = (rms_norm(q_a, g_q_a) @ w_q_b).reshape(B, T, MLA_HEADS, QK_NOPE_DIM + QK_ROPE_DIM)
    return q[..., :QK_NOPE_DIM], q[..., QK_NOPE_DIM:]


def mla_keys_values(kv_a, g_kv_a, w_kv_b):
    B, T, _ = kv_a.shape
    kv = (rms_norm(kv_a, g_kv_a) @ w_kv_b).reshape(B, T, MLA_HEADS, QK_NOPE_DIM + V_HEAD_DIM)
    return kv[..., :QK_NOPE_DIM], kv[..., QK_NOPE_DIM:]


def mla_attention(q_nope, q_rope, k_nope, k_rope, v):
    B, T, H, _ = q_nope.shape
    nb = T // Q_BLOCK
    scale = (QK_NOPE_DIM + QK_ROPE_DIM) ** -0.5
    qn = q_nope.reshape(B, nb, Q_BLOCK, H, QK_NOPE_DIM).transpose(1, 0, 2, 3, 4)
    qr = q_rope.reshape(B, nb, Q_BLOCK, H, QK_ROPE_DIM).transpose(1, 0, 2, 3, 4)

    def block(args):
        qn_b, qr_b = args
        s = (jnp.einsum('bqhd,bkhd->bhqk', qn_b, k_nope, preferred_element_type=jnp.float32)
             + jnp.einsum('bqhr,bkr->bhqk', qr_b, k_rope, preferred_element_type=jnp.float32))
        p = jax.nn.softmax(s * scale, axis=-1).astype(v.dtype)
        return jnp.einsum('bhqk,bkhd->bqhd', p, v)

    out = lax.map(block, (qn, qr))
    return out.transpose(1, 0, 2, 3, 4).reshape(B, T, H * V_HEAD_DIM)


def fourier_mix(u, w_fourier):
    B, T, _ = u.shape
    ug = u.reshape(B, T, FOURIER_GROUPS, FOURIER_GROUP_DIM).astype(jnp.float32)
    f = jnp.fft.fft2(ug, axes=(1, 3), norm="ortho").real.astype(u.dtype)
    return jnp.einsum('btgc,gcd->btgd', f, w_fourier).reshape(B, T, FOURIER_WIDTH)


def swiglu(h, w_gate, w_up, w_down):
    return (jax.nn.silu(h @ w_gate) * (h @ w_up)) @ w_down


def setup_inputs(seed: int = 0) -> dict:
    key = jax.random.key(seed)
    ks = jax.random.split(key, 20)
    f32 = jnp.float32

    def w(k, shape, fan_in):
        return jax.random.normal(k, shape, f32) * fan_in ** -0.5

    def gain(k, shape):
        return 1.0 + 0.1 * jax.random.normal(k, shape, f32)

    return {
        "x": jax.random.normal(ks[0], (BATCH, SEQ, D_MODEL), f32),
        "c": jax.random.normal(ks[1], (BATCH, D_MODEL), f32),
        "ctx": jax.random.normal(ks[2], (BATCH, CTX_LEN, D_MODEL), f32),
        "c_ctx": jax.random.normal(ks[3], (D_MODEL,), f32),
        "w_ada": w(ks[4], (DEPTH, D_MODEL, 6 * D_MODEL), D_MODEL),
        "b_ada": 0.02 * jax.random.normal(ks[5], (DEPTH, 6 * D_MODEL), f32),
        "g_pre_mix": gain(ks[6], (DEPTH, D_MODEL)),
        "g_post_mix": gain(ks[7], (DEPTH, D_MODEL)),
        "g_pre_ffn": gain(ks[8], (DEPTH, D_MODEL)),
        "g_post_ffn": gain(ks[9], (DEPTH, D_MODEL)),
        "w_in": w(ks[10], (DEPTH, D_MODEL, IN_WIDTH), D_MODEL),
        "g_q_a": gain(ks[11], (DEPTH, Q_LORA_RANK)),
        "w_q_b": w(ks[12], (DEPTH, Q_LORA_RANK, MLA_HEADS * (QK_NOPE_DIM + QK_ROPE_DIM)), Q_LORA_RANK),
        "g_kv_a": gain(ks[13], (DEPTH, KV_LORA_RANK)),
        "w_kv_b": w(ks[14], (DEPTH, KV_LORA_RANK, MLA_HEADS * (QK_NOPE_DIM + V_HEAD_DIM)), KV_LORA_RANK),
        "w_fourier": w(ks[15], (DEPTH, FOURIER_GROUPS, FOURIER_GROUP_DIM, FOURIER_GROUP_DIM), FOURIER_GROUP_DIM),
        "w_out": w(ks[16], (DEPTH, MIX_WIDTH, D_MODEL), MIX_WIDTH),
        "w_gate": w(ks[17], (DEPTH, D_MODEL, D_FF), D_MODEL),
        "w_up": w(ks[18], (DEPTH, D_MODEL, D_FF), D_MODEL),
        "w_down": w(ks[19], (DEPTH, D_FF, D_MODEL), D_FF),
    }


def reference(x, c, ctx, c_ctx, w_ada, b_ada, g_pre_mix, g_post_mix, g_pre_ffn, g_post_ffn,
              w_in, g_q_a, w_q_b, g_kv_a, w_kv_b, w_fourier, w_out, w_gate, w_up, w_down):
    n_lat = x.shape[1]
    ROWS = n_lat // GRID_W
    row = jnp.repeat(jnp.arange(ROWS, dtype=jnp.int32), GRID_W)
    col = jnp.tile(jnp.arange(GRID_W, dtype=jnp.int32), ROWS)

    for l in range(DEPTH):
        mod = jax.nn.silu(c) @ w_ada[l] + b_ada[l]
        sh_m, sc_m, gt_m, sh_f, sc_f, gt_f = jnp.split(mod[:, None, :], 6, axis=-1)
        mod_c = jax.nn.silu(c_ctx) @ w_ada[l] + b_ada[l]
        csh_m, csc_m, cgt_m, csh_f, csc_f, cgt_f = jnp.split(mod_c, 6, axis=-1)

        h_lat = modulate(rms_norm(x, g_pre_mix[l]), sh_m, sc_m)
        p_lat = h_lat @ w_in[l]
        u_f, q_a, kv_a, kr_raw = jnp.split(
            p_lat, [FOURIER_WIDTH, KV_COL, KV_COL + KV_LORA_RANK], axis=-1)

        h_ctx = modulate(rms_norm(ctx, g_pre_mix[l]), csh_m, csc_m)
        p_ctx_kv = h_ctx @ w_in[l][:, KV_COL:]
        kv_a_c, kr_c = p_ctx_kv[..., :KV_LORA_RANK], p_ctx_kv[..., KV_LORA_RANK:]

        kn_l, v_l = mla_keys_values(kv_a, g_kv_a[l], w_kv_b[l])
        kr_l = axial_rope(kr_raw, row, col)
        kn_c, v_c = mla_keys_values(kv_a_c, g_kv_a[l], w_kv_b[l])
        k_nope = jnp.concatenate([kn_c, kn_l], axis=1)
        k_rope = jnp.concatenate([kr_c, kr_l], axis=1)
        v_all = jnp.concatenate([v_c, v_l], axis=1)

        qn_l, qr_l = mla_queries(q_a, g_q_a[l], w_q_b[l])
        qr_l = axial_rope(qr_l, row, col)
        attn_l = mla_attention(qn_l, qr_l, k_nope, k_rope, v_all)
        four_l = fourier_mix(u_f, w_fourier[l])
        y_lat = jnp.concatenate([four_l, attn_l], axis=-1) @ w_out[l]

        if l + 1 < DEPTH:
            p_ctx_fq = h_ctx @ w_in[l][:, :KV_COL]
            u_f_c, q_a_c = p_ctx_fq[..., :FOURIER_WIDTH], p_ctx_fq[..., FOURIER_WIDTH:]
            qn_c, qr_c = mla_queries(q_a_c, g_q_a[l], w_q_b[l])
            attn_c = mla_attention(qn_c, qr_c, kn_c, kr_c, v_c)
            four_c = fourier_mix(u_f_c, w_fourier[l])
            y_ctx = jnp.concatenate([four_c, attn_c], axis=-1) @ w_out[l]
            ctx = ctx + cgt_m * rms_norm(y_ctx, g_post_mix[l])
            h2c = modulate(rms_norm(ctx, g_pre_ffn[l]), csh_f, csc_f)
            ctx = ctx + cgt_f * rms_norm(swiglu(h2c, w_gate[l], w_up[l], w_down[l]), g_post_ffn[l])

        x = x + gt_m * rms_norm(y_lat, g_post_mix[l])
        h2 = modulate(rms_norm(x, g_pre_ffn[l]), sh_f, sc_f)
        x = x + gt_f * rms_norm(swiglu(h2, w_gate[l], w_up[l], w_down[l]), g_post_ffn[l])
    return x
```

```python
import os
from contextlib import ExitStack
import numpy as np
import ml_dtypes
import concourse.bass as bass
import concourse.mybir as mybir
from concourse.bass_utils import run_bass_kernel_spmd

F32 = mybir.dt.float32
BF16 = mybir.dt.bfloat16
AF = mybir.ActivationFunctionType
ALU = mybir.AluOpType

S = 2048
C = 256
D = 1024
NK = S + C
DFF = 2816
NFC = 22
EPS = 1e-6
SCALE = 96.0 ** -0.5
KB = 1024


class Op:
    __slots__ = ("eng", "fn", "deps", "chan", "idx", "flag", "val")

    def __init__(self, eng, fn, deps, chan):
        self.eng = eng
        self.fn = fn
        self.deps = deps
        self.chan = chan
        self.flag = False
        self.val = None


def _flat(deps):
    out = []
    for x in deps:
        if x is None:
            continue
        if isinstance(x, (list, tuple)):
            out.extend(_flat(x))
        else:
            out.append(x)
    return out


class Prog:
    ENGS = ("pe", "act", "dve", "pool", "sp")

    def __init__(self, nc):
        self.nc = nc
        self.ops = []
        self.nchan = 0

    def chan(self):
        self.nchan += 1
        return self.nchan - 1

    def add(self, eng, fn, deps=(), chan=None):
        op = Op(eng, fn, _flat(deps), chan)
        op.idx = len(self.ops)
        self.ops.append(op)
        return op

    def emit(self, stack):
        nc = self.nc
        ops = self.ops

        def skip(d, op):
            return d.chan is None and d.eng == "pe" and op.eng == "pe" and op.chan is None

        for op in ops:
            for d in op.deps:
                assert d.idx < op.idx
                if not skip(d, op):
                    d.flag = True
            if op.chan is not None:
                op.flag = True
        eng_cnt = {e: 0 for e in self.ENGS}
        chan_cnt = {}
        for op in ops:
            if op.chan is not None:
                chan_cnt[op.chan] = chan_cnt.get(op.chan, 0) + 16
                op.val = chan_cnt[op.chan]
            elif op.flag:
                eng_cnt[op.eng] += 1
                op.val = eng_cnt[op.eng]
        sems = {}
        for e in self.ENGS:
            sems[("e", e)] = stack.enter_context(nc.semaphore("s_" + e))
        for c in sorted(chan_cnt):
            sems[("c", c)] = stack.enter_context(nc.semaphore("c_%d" % c))
        per = {e: [o for o in ops if o.eng == e] for e in self.ENGS}
        block = stack.enter_context(nc.Block())

        def run(e, eng):
            seen = {}
            for op in per[e]:
                need = {}
                for d in op.deps:
                    if not d.flag or skip(d, op):
                        continue
                    k = ("c", d.chan) if d.chan is not None else ("e", d.eng)
                    if need.get(k, 0) < d.val:
                        need[k] = d.val
                for k, v in need.items():
                    if seen.get(k, 0) >= v:
                        continue
                    eng.wait_ge(sems[k], v)
                    seen[k] = v
                ins = op.fn(eng)
                if op.flag:
                    k = ("c", op.chan) if op.chan is not None else ("e", op.eng)
                    ins.then_inc(sems[k], 16 if op.chan is not None else 1)
            last = {}
            for op in per[e]:
                if op.chan is not None:
                    last[op.chan] = op.val
            for c, v in last.items():
                if seen.get(("c", c), 0) < v:
                    eng.wait_ge(sems[("c", c)], v)

        block.tensor(lambda eng: run("pe", eng))
        block.scalar(lambda eng: run("act", eng))
        block.vector(lambda eng: run("dve", eng))
        block.gpsimd(lambda eng: run("pool", eng))
        block.sync(lambda eng: run("sp", eng))


class _Stop(Exception):
    pass


class Buf:
    def __init__(self):
        self.w = []
        self.r = []

    def wdeps(self):
        return list(self.r) + list(self.w)

    def wrote(self, op, more=False):
        if more:
            self.w.append(op)
        else:
            self.w = [op]
            self.r = []
        return op

    def rdeps(self):
        return list(self.w)

    def read(self, op):
        self.r.append(op)
        return op


def build(dbg=None):
    nc = bass.Bass("TRN2", target_bir_lowering=False)

    def din(name, shape, dt=F32):
        return nc.dram_tensor(name, list(shape), dt, kind="ExternalInput").ap()

    x_d = din("x", [S, D])
    ctx_d = din("ctx", [C, D])
    cvec_d = din("cvec", [2, D])
    cvecc_d = din("cvec_c", [128, 8, 2])
    g4c_d = din("g4_c", [128, 4, 8])
    badac_d = din("b_ada_c", [128, 48])
    gqc_d = din("gq_c", [128, 2])
    gkvc_d = din("gkv_c", [128, 1])
    wada_d = din("w_ada", [D, 6 * D])
    bada_d = din("b_ada", [6 * D])
    g4_d = din("g4", [4, D])
    win_d = din("w_in", [D, 960])
    gq_d = din("g_q_a", [256])
    wqb_d = din("w_q_b", [256, 1280])
    gkv_d = din("g_kv_a", [128])
    wkv_d = din("w_kv", [128, 1024])
    wf_d = din("w_f", [512, 128])
    wout_d = din("w_out", [D, D])
    wg_d = din("w_gate", [D, DFF])
    wu_d = din("w_up", [D, DFF])
    wd_d = din("w_down", [DFF, D])
    ident_d = din("ident", [128, 128], BF16)
    csc_d = din("csc", [128, 256], BF16)
    dft_d = din("dft", [16, 128, 1024], BF16)
    altk_d = din("altk", [1, 1024], BF16)
    altp_d = din("altp", [128, 2], BF16)
    rope_d = din("rope", [2, 128, NK])
    out_d = nc.dram_tensor("out", [S, D], F32, kind="ExternalOutput").ap()
    x1_d = nc.dram_tensor("x1s", [S, D], F32, kind="Internal").ap()
    dbg_d = {}
    if dbg:
        for k, shp in dbg.items():
            dbg_d[k] = nc.dram_tensor("dbg_" + k, list(shp[0]), shp[1], kind="ExternalOutput").ap()

    P = Prog(nc)
    nm = [0]

    def T(shape, dt, off):
        nm[0] += 1
        return nc.alloc_sbuf_tensor_at("t%d" % nm[0], list(shape), dt, offset=off + 16640)

    R0, R1, R2, R3, R4, R5, R6, R7, R8 = [k * KB for k in (0, 36, 72, 104, 136, 152, 168, 184, 196)]
    hT = T([128, 8, S], BF16, R0)
    KT = T([128, 8, NK], BF16, R0)
    h2T = T([128, 8, S], BF16, R0)
    wring = [T([128, 8, 1024], BF16, R1 + i * 16 * KB) for i in range(2)]
    Vt = T([128, 18, 8, 128], BF16, R1)
    wdn = T([128, NFC, 1024], BF16, R1)
    xst = [T([128, 1024], F32, R2 + i * 4 * KB) for i in range(4)]
    xs = [T([128, 1024], BF16, R2 + 16 * KB + i * 2 * KB) for i in range(4)]
    junk = T([128, 1024], BF16, R2 + 24 * KB)
    tmpb = T([128, 1024], F32, R6 + 8 * KB)
    sqT = T([128, 2, 512], BF16, R2 + 26 * KB)
    rbc = T([128, 512], F32, R2 + 28 * KB)
    rt1 = T([128, 512], F32, R6 + 4 * KB)
    rt2 = T([128, 512], F32, R6 + 6 * KB)
    QT = T([128, 8, S], BF16, R2)
    PQ2 = T([128, 16, 512], BF16, R2 + 16 * KB)
    Zt = T([128, 4, 128], BF16, R7 + 11 * KB)
    dring2 = [T([128, 512], BF16, R7 + i * KB) for i in range(8)]
    ropet = T([128, 2, NK], F32, R3)
    qnT = T([128, 2, S], BF16, R3 + 18 * KB)
    kvnT = T([128, NK], BF16, R3 + 26 * KB)
    catT = T([128, 8, S], BF16, R3)
    wst = [[T([128, 8, 512], BF16, R3 + (2 * i + j) * 8 * KB) for j in range(2)] for i in range(2)]
    UT = T([128, 4, S], BF16, R4)
    p7x = [T([128, 1024], F32, R4 + i * 4 * KB) for i in range(4)]
    p7xs2 = [T([128, 1024], BF16, R2 + (14 + 2 * i) * KB) for i in range(2)]
    p7junk = T([128, 1024], BF16, R2 + 18 * KB)
    p7t = [T([128, 1024], F32, R2 + (20 + 4 * i) * KB) for i in range(2)]
    p8x = [T([128, 1024], F32, R4 + i * 4 * KB) for i in range(2)]
    p8o = [T([128, 1024], F32, R4 + 8 * KB + i * 4 * KB) for i in range(2)]
    win = T([128, 8, 960], BF16, R5)
    accS = [T([128, 1024], F32, R5 + i * 4 * KB) for i in range(2)]
    denS = [T([128, 1024], F32, R5 + 8 * KB + i * 4 * KB) for i in range(2)]
    RF = T([128, 4, S], BF16, R5)
    actT = T([128, NFC, 1024], BF16, R5)
    hTc = T([128, 8, C], BF16, R6)
    wout = T([128, 8, 1024], BF16, R6)
    wqb = T([128, 2, 1280], BF16, R7)
    wkv = T([128, 1024], BF16, R7 + 5 * KB)
    krT = T([128, NK], BF16, R7 + 7 * KB)
    PT = [T([128, 1024], BF16, R7 + i * 2 * KB) for i in range(3)]
    dring = [T([128, 1024], BF16, R7 + i * 2 * KB) for i in range(6)]
    G1 = T([128, 1024], F32, R8)
    G2 = T([128, 1024], F32, R8 + 4 * KB)
    o = R8 + 8 * KB
    wf = T([128, 4, 128], BF16, o); o += 1024
    csc = T([128, 256], BF16, o); o += 512
    ident = T([128, 128], BF16, o); o += 256
    ones_bf = T([128, 128], BF16, o); o += 256
    csil = T([128, 8, 2], F32, o); o += 64
    sil_bf = T([128, 8, 2], BF16, o); o += 32
    gcols = T([128, 4, 8], F32, o); o += 128
    bcols = T([128, 48], F32, o); o += 192
    modc = T([128, 6, 8, 2], F32, o); o += 384
    Acol = T([128, 3, 8], F32, o); o += 96
    gqc = T([128, 2], F32, o); o += 32
    gkvc = T([128, 1], F32, o); o += 32
    epst = T([128, 1], F32, o); o += 32
    ssq = T([128, 4], F32, o); o += 32
    rstd = T([128, 4], F32, o); o += 32
    ssq2 = T([128, 4], F32, o); o += 32
    rstd2 = T([128, 4], F32, o); o += 32
    ssq8 = T([128, 8], F32, o); o += 32
    rstd8 = T([128, 8], F32, o); o += 32
    assert o <= 212700, o
    sil_rep = T([128, 8, 128], BF16, R6 + 12 * KB)
    ones_f = T([128, 128], F32, R5 + 15 * KB)
    rq1 = T([128, 512], F32, R5)
    rq2 = T([128, 512], F32, R5 + 2 * KB)
    Rq4 = T([128, 2, S], BF16, R5 + 4 * KB)
    rqb = [[rq1, rq2], [T([128, 512], F32, R5 + 12 * KB), T([128, 512], F32, R5 + 14 * KB)]]
    p8junk = T([128, 1024], BF16, R2 + 12 * KB)
    sg = [T([128, 512], F32, R2 + 8 * KB + i * 2 * KB) for i in range(2)]

    psum = nc.alloc_psum_tensor("ps", [128, 8, 512], F32)
    bank = [Buf() for _ in range(8)]

    def pb(b, n=512, p0=0, p1=128):
        return psum[p0:p1, b, 0:n]

    def pb2(b, p0=0, p1=128):
        return psum[p0:p1, b:b + 2, :].rearrange("p a b -> p (a b)")

    tp_bf = psum[:, 0:4, :].rearrange("p a b -> p (a b)").bitcast(BF16).rearrange("p (k t) -> p k t", k=8)

    def mm(out, lhsT, rhs, start, stop, deps=(), tp=None):
        def f(e):
            if tp is None:
                return e.matmul(out, lhsT=lhsT, rhs=rhs, start=start, stop=stop)
            return e.matmul(out, lhsT=lhsT, rhs=rhs, start=start, stop=stop, tile_position=tp)
        return P.add("pe", f, deps)

    def tr(out, in_, deps=()):
        return P.add("pe", lambda e: e.transpose(out=out, in_=in_, identity=ident[:]), deps)

    def act(out, in_, func, deps=(), scale=1.0, bias=None, accum=None):
        def f(e):
            kw = {}
            if bias is not None:
                kw["bias"] = bias
            if accum is not None:
                kw["accum_out"] = accum
            return e.activation(out=out, in_=in_, func=func, scale=scale, **kw)
        return P.add("act", f, deps)

    def ts(eng, out, in0, s1, s2, op0, op1=None, deps=()):
        def f(e):
            if op1 is None:
                return e.tensor_scalar(out=out, in0=in0, scalar1=s1, scalar2=None, op0=op0)
            return e.tensor_scalar(out=out, in0=in0, scalar1=s1, scalar2=s2, op0=op0, op1=op1)
        return P.add(eng, f, deps)

    def tt(eng, out, in0, in1, op, deps=()):
        return P.add(eng, lambda e: e.tensor_tensor(out=out, in0=in0, in1=in1, op=op), deps)

    def stt(eng, out, in0, scalar, in1, op0, op1, deps=()):
        return P.add(eng, lambda e: e.scalar_tensor_tensor(out=out, in0=in0, scalar=scalar, in1=in1, op0=op0, op1=op1), deps)

    def cp(eng, out, in_, deps=()):
        if eng == "act":
            return act(out, in_, AF.Copy, deps)
        return P.add(eng, lambda e: e.tensor_copy(out=out, in_=in_), deps)

    def memset(eng, ap, v, deps=()):
        return P.add(eng, lambda e: e.memset(ap, v), deps)

    def dma(q, out, in_, deps=(), chan=None, slow=False):
        if chan is None:
            chan = P.chan()
        def f(e):
            if slow:
                return e.dma_start(out=out, in_=in_, allow_slow_non_contiguous=True)
            return e.dma_start(out=out, in_=in_)
        return P.add(q, f, deps, chan)

    def tap(name, ap, deps):
        if dbg and name in dbg_d:
            dma("sp", dbg_d[name], ap, deps)

    def rstd_ops(dst, src, n, deps):
        a = act(dst, src, AF.Ln, deps, scale=1.0 / n, bias=epst[:])
        return act(dst, dst, AF.Exp, [a], scale=-0.5)

    STOP = int(os.environ.get('MK_STOP', '99'))
    SUB = int(os.environ.get('MK_SUB', '0'))
    try:
        ld_cvec = dma("sp", csil[:], cvecc_d)
        ld_ident = dma("sp", ident[:], ident_d)
        m_eps = memset("dve", epst[:], EPS)
        m_ob = memset("dve", ones_bf[:], 1.0)
        m_of = memset("dve", ones_f[:], 1.0)
        a_sil = act(sil_bf[:], csil[:], AF.Silu, [ld_cvec])
        srep = []
        for kc in range(8):
            srep.append(ts("dve", sil_rep[:, kc, :], ones_f[:], sil_bf[:, kc, 0:1], None, ALU.mult, deps=[a_sil, m_of]))

        wada_v = wada_d.rearrange("(kc p) n -> p kc n", p=128)
        ring_buf = [Buf(), Buf()]
        ring_chan = [P.chan(), P.chan()]
        order = [1, 0, 4, 3, 2, 5]
        ada_ld = {}

        def issue_ada_load(i):
            j = order[i]
            s = i % 2
            ada_ld[j] = ring_buf[s].wrote(dma("pool", wring[s][:], wada_v[:, :, j * 1024:(j + 1) * 1024],
                                               [ring_buf[s].wdeps(), w_in_ld if i >= 2 else None], chan=ring_chan[s]))

        tmpb_buf = Buf()

        def ada_cols(i):
            j = order[i]
            s = i % 2
            cv = psum[:, 7, 0:16].rearrange("p (n r) -> p n r", r=2)
            last = None
            w0 = bank[7].wdeps()
            for nn in range(8):
                for kc in range(8):
                    last = mm(cv[:, nn, :], wring[s][:, kc, nn * 128:(nn + 1) * 128], sil_bf[:, kc, :], kc == 0, kc == 7,
                              [ada_ld[j], a_sil, w0])
            bank[7].wrote(last)
            ring_buf[s].read(last)
            ev = []
            for r in range(2):
                ev.append(bank[7].read(tt("dve", modc[:, j, :, r], cv[:, :, r], bcols[:, j * 8:(j + 1) * 8], ALU.add,
                                          [last, ld_bc])))
            return ev

        def ada_gate(i, G, grow):
            j = order[i]
            s = i % 2
            l1 = tmpb_buf.wrote(dma("sp", tmpb[:], bada_d[j * 1024:(j + 1) * 1024].partition_broadcast(128), tmpb_buf.wdeps()))
            l2 = gpre[grow]
            res = []
            for nh in range(2):
                b = 5 + nh
                w0 = bank[b].wdeps()
                last = None
                for kc in range(8):
                    last = mm(pb(b), sil_rep[:, kc, :], wring[s][:, kc, nh * 512:(nh + 1) * 512], kc == 0, kc == 7,
                              [ada_ld[j], srep, w0])
                bank[b].wrote(last)
                ring_buf[s].read(last)
                sl = slice(nh * 512, (nh + 1) * 512)
                a = bank[b].read(tt("dve", tmpb[:, sl], pb(b), tmpb[:, sl], ALU.add, [last, l1]))
                res.append(tmpb_buf.read(tt("dve", G[:, sl], tmpb[:, sl], G[:, sl], ALU.mult, [a, l2])))
            return res

        xst_buf = [Buf() for _ in range(4)]
        xst_chan = [P.chan() for _ in range(4)]
        xs_buf = [Buf() for _ in range(4)]
        junk_buf = Buf()
        ssq_buf = [Buf(), Buf()]
        hT_done = []
        tile_ctr = [0]

        def norm1(src_d, ntile, par):
            sq_ops = []
            ms = []
            c0 = par * 4
            for j in range(ntile):
                s = j
                ld = xst_buf[s].wrote(dma("sp", xst[s][:], src_d[j * 128:(j + 1) * 128, :], xst_buf[s].wdeps(), chan=xst_chan[s]))
                a = act(junk[:], xst[s][:], AF.Square, [ld, junk_buf.wdeps(), ssq_buf[par].wdeps() if j == 0 else None], accum=ssq8[:, c0 + j:c0 + j + 1])
                junk_buf.wrote(a)
                xst_buf[s].read(a)
                sq_ops.append((a, ld))
            ssq_buf[par].wrote(sq_ops[-1][0])
            r = rstd_ops(rstd8[:, c0:c0 + ntile], ssq8[:, c0:c0 + ntile], D, [[q[0] for q in sq_ops], m_eps])
            for j in range(ntile):
                a, ld = sq_ops[j]
                m = ts("dve", xs[j][:], xst[j][:], rstd8[:, c0 + j:c0 + j + 1], None, ALU.mult, deps=[r, ld, xs_buf[j].wdeps()])
                xs_buf[j].wrote(m)
                xst_buf[j].read(m)
                ssq_buf[par].read(m)
                ms.append(m)
            return ms

        def norm2(ms, ntile, dstT, tok0, Ac, shc_j, shc_r, extra_deps, split=False):
            tps = []
            w0 = [bank[b].wdeps() for b in range(4)]
            for j in range(ntile):
                for kc in range(8):
                    t = tr(tp_bf[:, kc, j * 128:(j + 1) * 128], xs[j][:, kc * 128:(kc + 1) * 128], [ms[j], ld_ident, w0[kc // 2]])
                    tps.append(t)
                xs_buf[j].read(tps[-1])
            for b in range(4):
                bank[b].wrote(tps[-1])
            n = ntile * 128

            def evac(kcs):
                evs = []
                for kc in kcs:
                    e = act(dstT[:, kc, tok0:tok0 + n], tp_bf[:, kc, 0:n], AF.Identity, [tps[-1], extra_deps],
                            scale=Ac[:, kc:kc + 1], bias=modc[:, shc_j, kc, shc_r:shc_r + 1])
                    bank[kc // 2].read(e)
                    evs.append(e)
                return evs
            if split:
                return evac
            return evac(range(8))

        def kv_norm(b_kv, n, tok0, deps_mm):
            a = bank[b_kv].read(act(sqT[:, 0, 0:n], pb(b_kv, n), AF.Square, [deps_mm]))
            m = bank[7].wrote(mm(pb(7, n), ones_bf[:], sqT[:, 0, 0:n], True, True, [a, m_ob, bank[7].wdeps()]))
            r1 = bank[7].read(act(rbc[:, 0:n], pb(7, n), AF.Ln, [m, m_eps], scale=1.0 / 128, bias=epst[:]))
            r2 = act(rbc[:, 0:n], rbc[:, 0:n], AF.Exp, [r1], scale=-0.5)
            return bank[b_kv].read(stt("dve", kvnT[:, tok0:tok0 + n], pb(b_kv, n), gkvc[:, 0:1], rbc[:, 0:n], ALU.mult, ALU.mult,
                                       [r2, ld_gkv, deps_mm]))

        ms_g = {0: norm1(x_d[0:512, :], 4, 0)}
        ld_g4 = dma("sp", gcols[:], g4c_d)
        gpre = {}
        ld_bc = dma("sp", bcols[:], badac_d)
        ld_gq = dma("sp", gqc[:], gqc_d)
        ld_gkv = dma("sp", gkvc[:], gkvc_d)
        gpre = {1: dma("sp", G1[:], g4_d[1, :].partition_broadcast(128)), 3: dma("sp", G2[:], g4_d[3, :].partition_broadcast(128))}
        issue_ada_load(0)
        issue_ada_load(1)
        w_in_ld = dma("pool", win[:], win_d.rearrange("(kc p) n -> p kc n", p=128))
        ld_rope = dma("pool", ropet[:], rope_d.rearrange("c i t -> i c t"), [w_in_ld])
        ev_sc = ada_cols(0)
        issue_ada_load(2)
        ev_sh = ada_cols(1)
        issue_ada_load(3)
        A1ops = []
        for r in range(2):
            A1ops.append(stt("dve", Acol[:, r, :], modc[:, 1, :, r], 1.0, gcols[:, 0, :], ALU.add, ALU.mult, [ev_sc, ld_g4]))
        mod_ready = [A1ops, ev_sh]

        ld_wqb = dma("pool", wqb[:], wqb_d.rearrange("(c p) n -> p c n", p=128), [w_in_ld])
        ld_wkv = dma("pool", wkv[:], wkv_d, [w_in_ld])
        ld_csc = dma("pool", csc[:], csc_d, [w_in_ld])

        UT_ops = []
        qn_ops = []
        kr_ops = []
        kvn_ops = []
        flip = [0]

        def evac_copy(out, in_, deps):
            flip[0] ^= 1
            return cp("act" if flip[0] else "dve", out, in_, deps)

        sq_buf = Buf()
        rbc_buf = Buf()
        ev_g = {0: norm2(ms_g[0], 4, hT, 0, Acol[:, 0, :], 0, 0, mod_ready)}
        ms_c = None
        for gi in range(4):
            t0 = gi * 512
            ev = ev_g[gi]
            if gi + 1 < 4:
                ms_g[gi + 1] = norm1(x_d[t0 + 512:t0 + 1024, :], 4, (gi + 1) % 2)
            else:
                ms_c = norm1(ctx_d, 2, 0)
            for g in range(4):
                b = 4 + (g % 2)
                w0 = bank[b].wdeps()
                for kc in range(8):
                    last = mm(pb(b), win[:, kc, g * 128:(g + 1) * 128], hT[:, kc, t0:t0 + 512], kc == 0, kc == 7, [ev, w_in_ld, w0])
                bank[b].wrote(last)
                UT_ops.append(bank[b].read(evac_copy(UT[:, g, t0:t0 + 512], pb(b), [last])))
            if gi + 1 < 4:
                evf = norm2(ms_g[gi + 1], 4, hT, t0 + 512, Acol[:, 0, :], 0, 0, mod_ready, split=True)
                ev_g[gi + 1] = []
                ev_next = ev_g[gi + 1]
            else:
                evf = norm2(ms_c, 2, hTc, 0, Acol[:, 1, :], 0, 1, mod_ready, split=True)
                ev_c = []
                ev_next = ev_c
            qmm = []
            for c in range(2):
                b = 5 + c
                w0 = bank[b].wdeps()
                for kc in range(8):
                    last = mm(pb(b), win[:, kc, 512 + c * 128:512 + (c + 1) * 128], hT[:, kc, t0:t0 + 512], kc == 0, kc == 7, [ev, w_in_ld, w0])
                bank[b].wrote(last)
                qmm.append(last)
            sqs = []
            for c in range(2):
                sqs.append(bank[5 + c].read(act(sqT[:, c, :], pb(5 + c), AF.Square, [qmm[c], sq_buf.wdeps()])))
            ev_next += evf(range(0, 4))
            w0 = bank[7].wdeps()
            for c in range(2):
                last = mm(pb(7), ones_bf[:], sqT[:, c, :], c == 0, c == 1, [sqs, m_ob, w0])
            bank[7].wrote(last)
            sq_buf.wrote(sqs[-1]); sq_buf.read(last)
            r1 = bank[7].read(act(rbc[:], pb(7), AF.Ln, [last, m_eps, rbc_buf.wdeps()], scale=1.0 / 256, bias=epst[:]))
            r2 = rbc_buf.wrote(act(rbc[:], rbc[:], AF.Exp, [r1], scale=-0.5))
            for c in range(2):
                o_ = stt("dve", qnT[:, c, t0:t0 + 512], pb(5 + c), gqc[:, c:c + 1], rbc[:], ALU.mult, ALU.mult, [r2, ld_gq, qmm[c]])
                bank[5 + c].read(o_)
                rbc_buf.read(o_)
                qn_ops.append(o_)
            w0 = bank[4].wdeps()
            for kc in range(8):
                last = mm(pb(4), win[:, kc, 768:896], hT[:, kc, t0:t0 + 512], kc == 0, kc == 7, [ev, w_in_ld, w0])
            bank[4].wrote(last)
            a = bank[4].read(act(sqT[:, 0, :], pb(4), AF.Square, [last, sq_buf.wdeps()]))
            sq_buf.wrote(a)
            ev_next += evf(range(4, 8))
            m = bank[7].wrote(mm(pb(7), ones_bf[:], sqT[:, 0, :], True, True, [a, m_ob, bank[7].wdeps()]))
            sq_buf.read(m)
            r1 = bank[7].read(act(rbc[:], pb(7), AF.Ln, [m, m_eps, rbc_buf.wdeps()], scale=1.0 / 128, bias=epst[:]))
            r2 = rbc_buf.wrote(act(rbc[:], rbc[:], AF.Exp, [r1], scale=-0.5))
            o_ = stt("dve", kvnT[:, C + t0:C + t0 + 512], pb(4), gkvc[:, 0:1], rbc[:], ALU.mult, ALU.mult, [r2, ld_gkv, last])
            bank[4].read(o_); rbc_buf.read(o_)
            kvn_ops.append(o_)
            if gi == 2:
                ev_scf = ada_cols(2)
                issue_ada_load(4)
                ev_shf = ada_cols(3)
                issue_ada_load(5)
                A2op = stt("dve", Acol[:, 2, :], modc[:, 4, :, 0], 1.0, gcols[:, 2, :], ALU.add, ALU.mult, [ev_scf, ld_g4])
            krm = []
            for c in range(2):
                b = 5 + c
                w0 = bank[b].wdeps()
                for kc in range(8):
                    last = mm(pb(b, 512, 64, 96), win[:, kc, 896 + c * 32:928 + c * 32], hT[:, kc, t0:t0 + 512], kc == 0, kc == 7,
                              [ev, w_in_ld, w0], tp=(0, 64))
                bank[b].wrote(last)
                krm.append(last)
            hT_done.append(last)
            c1 = bank[5].read(cp("dve", rt1[64:96, :], pb(5, 512, 64, 96), [krm[0], kr_ops[-1:]]))
            c2 = bank[6].read(cp("act", rt2[64:96, :], pb(6, 512, 64, 96), [krm[1], kr_ops[-1:]]))
            k1 = tt("pool", rt1[64:96, :], rt1[64:96, :], ropet[64:96, 0, C + t0:C + t0 + 512], ALU.mult, [c1, ld_rope])
            k2 = tt("pool", rt2[64:96, :], rt2[64:96, :], ropet[64:96, 1, C + t0:C + t0 + 512], ALU.mult, [c2, ld_rope, k1])
            kr_ops.append(tt("pool", krT[64:96, C + t0:C + t0 + 512], rt1[64:96, :], rt2[64:96, :], ALU.add, [k1, k2]))

        last = None
        w0 = bank[4].wdeps()
        for kc in range(8):
            last = mm(pb(4, C), win[:, kc, 768:896], hTc[:, kc, :], kc == 0, kc == 7, [ev_c, w_in_ld, w0])
        bank[4].wrote(last)
        a = bank[4].read(act(sqT[:, 0, 0:C], pb(4, C), AF.Square, [last, sq_buf.wdeps()]))
        sq_buf.wrote(a)
        m = bank[7].wrote(mm(pb(7, C), ones_bf[:], sqT[:, 0, 0:C], True, True, [a, m_ob, bank[7].wdeps()]))
        sq_buf.read(m)
        r1 = bank[7].read(act(rbc[:, 0:C], pb(7, C), AF.Ln, [m, m_eps, rbc_buf.wdeps()], scale=1.0 / 128, bias=epst[:]))
        r2 = rbc_buf.wrote(act(rbc[:, 0:C], rbc[:, 0:C], AF.Exp, [r1], scale=-0.5))
        o_ = stt("dve", kvnT[:, 0:C], pb(4, C), gkvc[:, 0:1], rbc[:, 0:C], ALU.mult, ALU.mult, [r2, ld_gkv, last])
        bank[4].read(o_); rbc_buf.read(o_)
        kvn_ops.append(o_)
        w0 = bank[5].wdeps()
        for kc in range(8):
            last = mm(pb(5, C, 64, 96), win[:, kc, 896:928], hTc[:, kc, :], kc == 0, kc == 7, [ev_c, w_in_ld, w0], tp=(0, 64))
        bank[5].wrote(last)
        kr_ops.append(bank[5].read(cp("dve", krT[64:96, 0:C], pb(5, C, 64, 96), [last])))
        hTc_done = last
        tap("hT", hT[:], UT_ops)
        tap("modc", modc[:], [UT_ops, ev_scf, ev_shf])
        tap("Acol", Acol[:], [UT_ops, A2op])
        tap("UT", UT[:], UT_ops)
        tap("qnT", qnT[:], qn_ops)
        tap("kvnT", kvnT[:], kvn_ops)
        tap("krT", krT[64:96, :], kr_ops)

        if STOP == 4:
            raise _Stop
        ph3_done = [hT_done, hTc_done, kvn_ops, kr_ops, qn_ops[-1]]
        KT_w = []
        rot = [0]

        def nextbank(n=6):
            b = rot[0] % n
            rot[0] += 1
            return b

        QT_w = []
        V_w = []
        rt_buf = [Buf(), Buf()]
        jobsD = []
        jobsA = []
        rq_state = {0: [], 1: []}

        def job_qrope(g, tq):
            def f():
                t0 = tq * 512
                bs_ = []
                for v in range(2):
                    b = nextbank()
                    w0 = bank[b].wdeps()
                    for rc in range(2):
                        last = mm(pb(b), wqb[:, rc, 768 + v * 256 + g * 128:768 + v * 256 + (g + 1) * 128], qnT[:, rc, t0:t0 + 512],
                                  rc == 0, rc == 1, [qn_ops, ld_wqb, w0])
                    bank[b].wrote(last)
                    bs_.append((b, last))
                par_ = (g * 4 + tq) % 2
                ra, rb_ = rqb[par_]
                k1 = bank[bs_[0][0]].read(stt("dve", ra[:], pb(bs_[0][0]), SCALE, ropet[:, 0, C + t0:C + t0 + 512], ALU.mult, ALU.mult,
                                              [bs_[0][1], rt_buf[par_].wdeps(), ld_rope, ph3_done]))
                k2 = bank[bs_[1][0]].read(stt("dve", rb_[:], pb(bs_[1][0]), SCALE, ropet[:, 1, C + t0:C + t0 + 512], ALU.mult, ALU.mult,
                                              [bs_[1][1], rt_buf[par_].wdeps(), ld_rope, ph3_done]))
                rt_buf[par_].wrote(k1); rt_buf[par_].wrote(k2, more=True)
                o_ = tt("pool", Rq4[:, g, t0:t0 + 512], ra[:], rb_[:], ALU.add, [k1, k2])
                rt_buf[par_].read(o_)
                rq_state[g].append(o_)
                if tq == 3:
                    qch = P.chan()
                    for hl in range(4):
                        QT_w.append(dma("sp", QT[64:96, 4 * g + hl, :], Rq4[hl * 32:(hl + 1) * 32, g, :],
                                        [rq_state[g], ph3_done, kr_ops, kvn_ops], chan=qch))
            return f

        def job_knope(h, c0):
            def f():
                n = min(512, NK - c0)
                b = nextbank()
                m = bank[b].wrote(mm(pb(b, n, 0, 64), wkv[:, h * 64:(h + 1) * 64], kvnT[:, c0:c0 + n], True, True,
                                     [kvn_ops, ld_wkv, bank[b].wdeps()]))
                KT_w.append(bank[b].read(cp("dve", KT[0:64, h, c0:c0 + n], pb(b, n, 0, 64), [m, ph3_done])))
                if c0 == 0:
                    KT_w.append(cp("dve", KT[64:96, h, :], krT[64:96, :], [kr_ops, ph3_done]))
            return f

        def job_v(tkt):
            def f():
                b = nextbank()
                m = bank[b].wrote(mm(pb(b), kvnT[:, tkt * 128:(tkt + 1) * 128], wkv[:, 512:1024], True, True, [kvn_ops, ld_wkv, bank[b].wdeps()]))
                pv = pb(b).rearrange("p (h d) -> p h d", h=8)
                V_w.append(bank[b].read(cp("act", Vt[:, tkt, 0:8:2, 0:64], pv[:, 0:8:2, :], [m, ph3v])))
                V_w.append(bank[b].read(cp("act", Vt[:, tkt, 1:8:2, 64:128], pv[:, 1:8:2, :], [m, ph3v])))
            return f

        def job_qnope(h, tq, eng):
            def f():
                t0 = tq * 512
                b = nextbank()
                w0 = bank[b].wdeps()
                for rc in range(2):
                    last = mm(pb(b, 512, 0, 64), wqb[:, rc, h * 96:h * 96 + 64], qnT[:, rc, t0:t0 + 512], rc == 0, rc == 1, [qn_ops, ld_wqb, w0])
                ma = bank[b].wrote(last)
                if eng == "act":
                    e = act(QT[0:64, h, t0:t0 + 512], pb(b, 512, 0, 64), AF.Copy, [ma, ph3_done, kr_ops, kvn_ops], scale=SCALE)
                else:
                    e = ts("dve", QT[0:64, h, t0:t0 + 512], pb(b, 512, 0, 64), SCALE, None, ALU.mult, deps=[ma, ph3_done, kr_ops, kvn_ops])
                QT_w.append(bank[b].read(e))
            return f

        g1_ops = ada_gate(4, G1, 1)
        g2_ops = ada_gate(5, G2, 3)
        ring_last = [ring_buf[0].wdeps(), ring_buf[1].wdeps(), g1_ops, g2_ops]
        ph3v = [ph3_done, ring_last]
        ms_v = []
        for g in range(2):
            for tq in range(4):
                jobsD.append(job_qrope(g, tq))
        for h in range(8):
            for c0 in range(0, NK, 512):
                jobsD.append(job_knope(h, c0))
        for tkt in range(18):
            jobsA.append(job_v(tkt))
        qi = 0
        for h in range(8):
            for tq in range(4):
                qi += 1
                if qi % 8 == 0:
                    jobsD.append(job_qnope(h, tq, "dve"))
                else:
                    jobsA.append(job_qnope(h, tq, "act"))
        ia = ib = 0
        while ia < len(jobsD) or ib < len(jobsA):
            if ia < len(jobsD):
                jobsD[ia](); ia += 1
                if ia == 8:
                    ms_v += [memset("pool", Vt[:, :, 0:8:2, 64:128], 1.0, [ph3v]),
                             memset("pool", Vt[:, :, 1:8:2, 0:64], 1.0, [ph3v])]
            if ib < len(jobsA):
                jobsA[ib](); ib += 1
        V_w.append(ms_v)
        tap("KT", KT[0:96, :, :], KT_w)
        tap("QT", QT[0:96, :, :], QT_w)
        tap("Vt", Vt[:], V_w)
        ph4_done = [KT_w, V_w, QT_w]

        if STOP == 5:
            raise _Stop
        ld_wout = dma("pool", wout[:], wout_d.rearrange("(kc p) n -> p kc n", p=128), [ph3_done, kr_ops, ring_last])

        PT_buf = [Buf() for _ in range(3)]
        accS_buf = [Buf(), Buf()]
        denS_buf = [Buf(), Buf()]
        den_chan = [P.chan(), P.chan()]
        cat_attn = []
        its = [(h, half, tkt) for h in range(8) for half in range(2) for tkt in range(18)]
        nit = len(its)
        exp_ops = {}
        dtab = T([128, 16, 1024], BF16, R0)
        ld_tab_early = []

        def emit_S(i):
            h, half, tkt = its[i]
            tq0 = half * 1024
            sb_ = (i % 2) * 2
            w0 = [bank[sb_].wdeps(), bank[sb_ + 1].wdeps()]
            for j in range(2):
                m = mm(pb(sb_ + j), KT[0:96, h, tkt * 128:(tkt + 1) * 128], QT[0:96, h, tq0 + j * 512:tq0 + (j + 1) * 512], True, True,
                       [ph4_done if i < 2 else None, w0[j]])
                bank[sb_ + j].wrote(m)
            s = i % 3
            e = act(PT[s][:], pb2(sb_), AF.Exp, [m, PT_buf[s].wdeps()])
            bank[sb_].read(e); bank[sb_ + 1].read(e)
            PT_buf[s].wrote(e)
            exp_ops[i] = e
            exp_ops["lastS"] = m
            if half == 1 and tkt == 17 and h in (1, 3, 5):
                c_ = h // 2
                ld_tab_early.append(dma("sp", dtab[:, 4 * c_:4 * c_ + 4, :], dft_d[4 * c_:4 * c_ + 4].rearrange("t p k -> p t k"), [m]))

        def emit_PV(i):
            h, half, tkt = its[i]
            tq0 = half * 1024
            ab = 4 + 2 * ((h * 2 + half) % 2)
            s = i % 3
            e = exp_ops.pop(i)
            if tkt == 0:
                emit_PV.wacc = [bank[ab].wdeps(), bank[ab + 1].wdeps()]
            for j in range(2):
                lastpv = mm(pb(ab + j), Vt[:, tkt, h, :], PT[s][:, j * 512:(j + 1) * 512], tkt == 0, tkt == 17,
                            [e, emit_PV.wacc[j] if tkt == 0 else None])
            PT_buf[s].read(lastpv)
            if tkt == 17:
                par = h % 2
                nr0 = par * 64
                dr0 = (1 - par) * 64
                bank[ab].wrote(lastpv); bank[ab + 1].wrote(lastpv)
                q = (h * 2 + half) % 2
                ev = cp("dve", accS[q][:], pb2(ab), [lastpv, accS_buf[q].wdeps(), ph4_done])
                bank[ab].read(ev); bank[ab + 1].read(ev)
                accS_buf[q].wrote(ev)
                d = dma("sp", denS[q][nr0:nr0 + 64, :], accS[q][dr0:dr0 + 64, :], [ev, denS_buf[q].wdeps()], chan=den_chan[q])
                accS_buf[q].read(d)
                denS_buf[q].wrote(d)
                rc_ = P.add("dve", lambda e_, q=q, nr0=nr0: e_.reciprocal(out=denS[q][nr0:nr0 + 64, :], in_=denS[q][nr0:nr0 + 64, :]), [d])
                o_ = tt("pool", catT[nr0:nr0 + 64, 4 + h // 2, tq0:tq0 + 1024], accS[q][nr0:nr0 + 64, :], denS[q][nr0:nr0 + 64, :], ALU.mult,
                        [rc_, ph4_done])
                accS_buf[q].read(o_); denS_buf[q].read(o_)
                cat_attn.append(o_)
            return lastpv

        def emit_fold():
            up_rev = UT[:, :, 2047:1024:-1]
            lo = UT[:, :, 1:1024]
            f1 = tt("dve", up_rev, lo, up_rev, ALU.subtract, [UT_ops])
            f2 = stt("dve", lo, lo, 2.0, up_rev, ALU.mult, ALU.subtract, [f1])
            zz = memset("pool", Zt[:], 0.0, [ph4_done])
            z1 = cp("dve", Zt[:, :, 0:1], UT[:, :, 1024:1025], [zz, UT_ops])
            z2 = memset("dve", UT[:, :, 1024:1025], 0.0, [z1])
            fold_ops = [f1, f2, z1, z2]
            return fold_ops

        emit_S(0)
        emit_S(1)
        for i in range(nit):
            if i + 2 < nit:
                emit_S(i + 2)
            lastpv = emit_PV(i)
            if i == 8:
                fold_ops = emit_fold()
        tap("attnT", catT[:, 4:8, :], cat_attn)
        ph5_done = [cat_attn, lastpv]

        if STOP == 6:
            raise _Stop
        altk = T([128, 1024], BF16, R0 + 32 * KB)
        altp = T([128, 2], BF16, R0 + 34 * KB)
        kt_dead = [exp_ops["lastS"]]
        ld_tab = ld_tab_early + [dma("sp", dtab[:, 12:16, :], dft_d[12:16].rearrange("t p k -> p t k"), kt_dead)]
        assert len(ld_tab) == 4
        ld_altk = dma("sp", altk[0:1, :], altk_d, kt_dead)
        ld_altp = dma("sp", altp[:], altp_d, kt_dead)
        ld_wf = dma("pool", wf[:], wf_d.rearrange("(g m) d -> m g d", m=128))
        PQ_w = []
        for tt_ in range(16):
            b = nextbank(4)
            w0 = bank[b].wdeps()
            sinp = tt_ >= 8
            for g in range(4):
                o_ap = psum[:, b, g * 128:(g + 1) * 128]
                last = mm(o_ap, UT[:, g, tt_ * 128:(tt_ + 1) * 128], csc[:, 128:256] if sinp else csc[:, 0:128], True, tt_ != 8,
                          [fold_ops, ld_csc, w0])
                if tt_ == 8:
                    last = mm(o_ap, Zt[:, g, :], csc[:, 0:128], False, True, [fold_ops])
            bank[b].wrote(last)
            e = cp("act", PQ2[:, tt_, :], pb(b), [last, exp_ops["lastS"]])
            bank[b].read(e)
            PQ_w.append(e)
        NR = 8
        dr_buf = [Buf() for _ in range(NR)]
        dr_chan = [P.chan() for _ in range(NR)]
        RF_w = []
        di = 0
        tmpS = [T([128, 512], F32, R4), T([128, 512], F32, R4 + 2 * KB)]
        pq_last = last
        tmpS_buf = [Buf(), Buf()]
        bn = 0
        wn = bank[bn].wdeps()
        for g in range(4):
            for tt_ in range(8):
                last = mm(psum[:, bn, g:g + 1], PQ2[:, tt_, g * 128:(g + 1) * 128], altp[:, 0:1], tt_ == 0, False, [ld_altp, wn, PQ_w])
            last = mm(psum[:, bn, g:g + 1], PQ2[0:1, 8, g * 128:(g + 1) * 128], altp[0:1, 1:2], False, True, [ld_altp])
        bank[bn].wrote(last)
        RF_w.append(bank[bn].read(cp("dve", RF[:, :, 1024:1025], psum[:, bn, 0:4].rearrange("p (g o) -> p g o", o=1), [last, ph5_done])))
        units = [(kq, gp) for kq in range(2) for gp in range(2)]
        for ui, (kq, gp) in enumerate(units):
            ab = 4 if ui % 2 == 0 else 0
            w0 = [bank[ab + i].wdeps() for i in range(4)]
            for tt_ in range(16):
                for gl in range(2):
                    g = gp * 2 + gl
                    b = ab + gl if tt_ < 8 else ab + 2 + gl
                    first = tt_ in (0, 8)
                    last = mm(pb(b), PQ2[:, tt_, g * 128:(g + 1) * 128], dtab[:, tt_, kq * 512:(kq + 1) * 512], first, tt_ == 15,
                              [ld_tab[tt_ // 4], PQ_w if ui < 2 else None, w0[b - ab] if first else None])
            lastS_ = last
            for gl in range(2):
                g = gp * 2 + gl
                last = mm(pb(ab + gl), PQ2[0:1, 8, g * 128:(g + 1) * 128], altk[0:1, kq * 512:(kq + 1) * 512], False, True, [ld_altk])
            for i in range(4):
                bank[ab + i].wrote(last)
            k0 = kq * 512
            for gl in range(2):
                g = gp * 2 + gl
                bc, bs = ab + gl, ab + 2 + gl
                q = (ui * 2 + gl) % 2
                cS = bank[bs].read(act(tmpS[q][:], pb(bs), AF.Copy, [lastS_, tmpS_buf[q].wdeps(), pq_last]))
                tmpS_buf[q].wrote(cS)
                e1 = bank[bc].read(tt("dve", RF[:, g, k0:k0 + 512], pb(bc), tmpS[q][:], ALU.add, [last, cS, ph5_done]))
                if kq == 0:
                    e2 = bank[bc].read(tt("dve", RF[:, g, 2047:1536:-1], psum[:, bc, 1:512], tmpS[q][:, 1:512], ALU.subtract, [last, cS, ph5_done]))
                else:
                    e2 = bank[bc].read(tt("dve", RF[:, g, 1536:1024:-1], pb(bc), tmpS[q][:], ALU.subtract, [last, cS, ph5_done]))
                tmpS_buf[q].read(e1); tmpS_buf[q].read(e2)
                RF_w += [e1, e2]
        tap("RF", RF[:], RF_w)
        cat_f = []
        for g in range(4):
            for kq in range(4):
                b = nextbank(4)
                m = bank[b].wrote(mm(pb(b), wf[:, g, :], RF[:, g, kq * 512:(kq + 1) * 512], True, True, [RF_w, ld_wf, bank[b].wdeps()]))
                cat_f.append(bank[b].read(evac_copy(catT[:, g, kq * 512:(kq + 1) * 512], pb(b), [m, ph4_done])))
        tap("fourT", catT[:, 0:4, :], cat_f)
        ph6_done = [cat_f, last]

        if STOP == 7:
            raise _Stop

        wgv = wg_d.rearrange("(kc p) n -> p kc n", p=128)
        wuv = wu_d.rearrange("(kc p) n -> p kc n", p=128)
        wst3 = [wst[0], wst[1], [T([128, 8, 512], BF16, R5), T([128, 8, 512], BF16, R5 + 8 * KB)]]
        wst_buf = [Buf(), Buf(), Buf()]
        wst_chan = [[P.chan(), P.chan()] for _ in range(3)]
        blk_ld = {}

        def issue_blk(k, deps):
            fb = k % 6
            ncol = 512 if fb < 5 else 256
            s_ = 2 if k == 0 else k % 2
            wd0 = wst_buf[s_].wdeps()
            lg = dma("pool", wst3[s_][0][:, :, 0:ncol], wgv[:, :, fb * 512:fb * 512 + ncol], [wd0, deps], chan=wst_chan[s_][0])
            lu = dma("pool", wst3[s_][1][:, :, 0:ncol], wuv[:, :, fb * 512:fb * 512 + ncol], [wd0, deps], chan=wst_chan[s_][1])
            wst_buf[s_].wrote(lg); wst_buf[s_].wrote(lu, more=True)
            blk_ld[k] = (lg, lu)

        issue_blk(0, [ph6_done, ph5_done])

        p7x_buf = [Buf() for _ in range(4)]
        p7x_chan = [P.chan() for _ in range(4)]
        p7t_buf = [Buf(), Buf()]
        p7xs_buf = [Buf(), Buf()]
        p7j_buf = Buf()
        st_chan = [P.chan() for _ in range(4)]
        x1_st = []
        h2_w = []
        tp7v = [psum[:, 6:8, :].rearrange("p a b -> p (a b)").bitcast(BF16).rearrange("p (k t) -> p k t", k=8)]
        st7 = {}

        def p7A(t):
            s_ = t % 4
            ld = p7x_buf[s_].wrote(dma("sp", p7x[s_][:], x_d[t * 128:(t + 1) * 128, :], [p7x_buf[s_].wdeps(), ph6_done], chan=p7x_chan[s_]))
            yb = (t % 3) * 2
            wy = [bank[yb].wdeps(), bank[yb + 1].wdeps()]
            for nh in range(2):
                for ch in range(8):
                    last = mm(pb(yb + nh), catT[:, ch, t * 128:(t + 1) * 128], wout[:, ch, nh * 512:(nh + 1) * 512], ch == 0, ch == 7,
                              [ph5_done, ph6_done, ld_wout, wy[nh]])
                bank[yb + nh].wrote(last)
            st7[t] = [ld, last]
            return last

        def p7B1(t):
            s_ = t % 4
            c = t % 4
            q = t % 2
            yb = (t % 3) * 2
            ld, last = st7[t]
            a = act(p7junk[:], pb2(yb), AF.Square, [last, p7j_buf.wdeps()], accum=ssq2[:, c:c + 1])
            bank[yb].read(a); bank[yb + 1].read(a)
            p7j_buf.wrote(a)
            r = rstd_ops(rstd2[:, c:c + 1], ssq2[:, c:c + 1], D, [a, m_eps])
            t1 = stt("dve", p7t[q][:], pb2(yb), rstd2[:, c:c + 1], G1[:], ALU.mult, ALU.mult, [r, g1_ops, p7t_buf[q].wdeps()])
            bank[yb].read(t1); bank[yb + 1].read(t1)
            x1 = tt("dve", p7x[s_][:], p7x[s_][:], p7t[q][:], ALU.add, [t1, ld])
            p7t_buf[q].wrote(t1); p7t_buf[q].read(x1)
            st = dma("sp", x1_d[t * 128:(t + 1) * 128, :], p7x[s_][:], [x1], chan=st_chan[s_])
            x1_st.append(st)
            p7x_buf[s_].read(st)
            st7[t] = x1

        def p7B2(t):
            s_ = t % 4
            c = t % 4
            q = t % 2
            x1 = st7.pop(t)
            a2 = act(p7junk[:], p7x[s_][:], AF.Square, [x1, p7j_buf.wdeps()], accum=ssq[:, c:c + 1])
            p7j_buf.wrote(a2)
            r2 = rstd_ops(rstd[:, c:c + 1], ssq[:, c:c + 1], D, [a2, m_eps])
            m = ts("dve", p7xs2[q][:], p7x[s_][:], rstd[:, c:c + 1], None, ALU.mult, deps=[r2, p7xs_buf[q].wdeps()])
            p7xs_buf[q].wrote(m)
            p7x_buf[s_].read(m); p7x_buf[s_].read(a2)
            return m

        pair_w0 = {}

        tl7 = {}

        def p7C(t, m):
            q = t % 2
            j2 = t % 2
            if j2 == 0:
                pair_w0[0] = [bank[6].wdeps(), bank[7].wdeps()]
            for kc in range(8):
                tlast = tr(tp7v[0][:, kc, j2 * 128:(j2 + 1) * 128], p7xs2[q][:, kc * 128:(kc + 1) * 128], [m, pair_w0[0][kc // 4]])
            p7xs_buf[q].read(tlast)
            tl7[t] = tlast
            if j2 == 1:
                bank[6].wrote(tlast); bank[7].wrote(tlast)

        def p7E(t):
            tlast = tl7[t]
            act_last = None
            for kc in (0, 1, 4, 5, 6, 7, 2, 3):
                o_ap = h2T[:, kc, (t - 1) * 128:(t + 1) * 128]
                if kc < 2:
                    e = act(o_ap, tp7v[0][:, kc, :], AF.Identity, [tlast, A2op, ev_shf, ph5_done],
                            scale=Acol[:, 2, kc:kc + 1], bias=modc[:, 3, kc, 0:1])
                    act_last = e
                else:
                    e = ts("dve", o_ap, tp7v[0][:, kc, :], Acol[:, 2, kc:kc + 1], modc[:, 3, kc, 0:1], ALU.mult, ALU.add,
                           deps=[tlast, A2op, ev_shf, ph5_done, act_last if kc < 4 else None])
                bank[6 + kc // 4].read(e)
                h2_w.append(e)

        last = p7A(0)
        last = p7A(1)
        last = p7A(2)
        p7B1(0)
        p7B1(1)
        for t in range(16):
            if t + 3 < 16:
                last = p7A(t + 3)
            m_ = p7B2(t)
            p7C(t, m_)
            if t + 2 < 16:
                p7B1(t + 2)
            if t % 2 == 1:
                p7E(t)
        tap("h2T", h2T[:], h2_w)
        ph7_done = [h2_w, x1_st, last]

        if STOP == 8:
            raise _Stop
        sg_buf = [Buf(), Buf()]
        p8x_buf = [Buf(), Buf()]
        p8x_chan = [P.chan(), P.chan()]
        p8o_buf = [Buf(), Buf()]
        out_chan = [P.chan(), P.chan()]
        out_st = []
        cnt8 = {"sgi": 0, "gset": 0}
        blocks = [(half, fb) for half in range(2) for fb in range(6)]
        act_w = {0: [], 1: []}

        def compute_block(k):
            half, fb = blocks[k]
            tk0 = half * 1024
            ncol = 512 if fb < 5 else 256
            s_ = 2 if k == 0 else k % 2
            lg, lu = blk_ld[k]
            for fi in range(ncol // 128):
                fc = fb * 4 + fi
                for tq in range(2):
                    gb = (cnt8["gset"] % 3) * 2
                    cnt8["gset"] += 1
                    w0 = [bank[gb].wdeps(), bank[gb + 1].wdeps()]
                    for kc in range(8):
                        mg = mm(pb(gb), wst3[s_][0][:, kc, fi * 128:(fi + 1) * 128], h2T[:, kc, tk0 + tq * 512:tk0 + (tq + 1) * 512], kc == 0, kc == 7,
                                [lg, ph7_done, w0[0]])
                    bank[gb].wrote(mg)
                    for kc in range(8):
                        mu = mm(pb(gb + 1), wst3[s_][1][:, kc, fi * 128:(fi + 1) * 128], h2T[:, kc, tk0 + tq * 512:tk0 + (tq + 1) * 512], kc == 0, kc == 7,
                                [lu, w0[1]])
                    bank[gb + 1].wrote(mu)
                    wst_buf[s_].read(mu)
                    q = cnt8["sgi"] % 2
                    cnt8["sgi"] += 1
                    a = bank[gb].read(act(sg[q][:], pb(gb), AF.Silu, [mg, sg_buf[q].wdeps(), ph7_done]))
                    sg_buf[q].wrote(a)
                    o_ = tt("dve", actT[:, NFC - 1 - fc, tq * 512:(tq + 1) * 512], sg[q][:], pb(gb + 1), ALU.mult,
                            [a, mu, ph7_done, down_last])
                    bank[gb + 1].read(o_)
                    sg_buf[q].read(o_)
                    act_w[half].append(o_)

        def down_half(half):
            last = None
            for tl in range(8):
                tt_ = half * 8 + tl
                s = tt_ % 2
                ld = p8x_buf[s].wrote(dma("sp", p8x[s][:], x1_d[tt_ * 128:(tt_ + 1) * 128, :], [p8x_buf[s].wdeps(), x1_st, ph7_done], chan=p8x_chan[s]))
                zb = (tl % 2) * 2
                wz = [bank[zb].wdeps(), bank[zb + 1].wdeps()]
                for nh in range(2):
                    for fc in range(NFC):
                        last = mm(pb(zb + nh), actT[:, NFC - 1 - fc, tl * 128:(tl + 1) * 128], wdn[:, fc, nh * 512:(nh + 1) * 512], fc == 0, fc == NFC - 1,
                                  [act_w[half], ld_wd, wz[nh]])
                    bank[zb + nh].wrote(last)
                a = act(p8junk[:], pb2(zb), AF.Square, [last, p7j_buf.wdeps(), p8prev[s], h2_w], accum=ssq2[:, s:s + 1])
                bank[zb].read(a); bank[zb + 1].read(a)
                p7j_buf.wrote(a)
                r = rstd_ops(rstd2[:, s:s + 1], ssq2[:, s:s + 1], D, [a, m_eps])
                t1 = stt("dve", p8o[s][:], pb2(zb), rstd2[:, s:s + 1], G2[:], ALU.mult, ALU.mult, [r, g2_ops, p8o_buf[s].wdeps()])
                p8prev[s] = [r, t1]
                bank[zb].read(t1); bank[zb + 1].read(t1)
                o_ = tt("pool", p8o[s][:], p8o[s][:], p8x[s][:], ALU.add, [t1, ld])
                p8x_buf[s].read(o_)
                st = dma("sp", out_d[tt_ * 128:(tt_ + 1) * 128, :], p8o[s][:], [o_], chan=out_chan[s])
                p8o_buf[s].wrote(t1); p8o_buf[s].read(st)
                out_st.append(st)
            return last

        p8prev = [[], []]
        down_last = None
        issue_blk(1, [ph7_done])
        ld_wd = []
        wd_v = wd_d.rearrange("(fc p) n -> p fc n", p=128)
        wd_cut = [0, 6, 12, 17, NFC]
        for k in range(12):
            compute_block(k)
            if k + 2 < 12:
                issue_blk(k + 2, [ph7_done])
            if k < 4:
                ld_wd.append(dma("pool", wdn[:, wd_cut[k]:wd_cut[k + 1], :], wd_v[:, wd_cut[k]:wd_cut[k + 1], :], [ph5_done, ph6_done]))
            if k == 5:
                down_last = down_half(0)
        down_half(1)

    except _Stop:
        pass
    print('nchan', P.nchan, 'nops', len(P.ops))
    with ExitStack() as st:
        P.emit(st)
    return nc


def _consts():
    bf = ml_dtypes.bfloat16
    ident = np.eye(128, dtype=np.float32).astype(bf)
    cm = np.arange(128)
    angc = 2 * np.pi * np.outer(cm, cm) / 128.0
    csc = np.concatenate([np.cos(angc), np.sin(angc)], axis=1) / 512.0
    t = np.arange(S, dtype=np.int64)
    tk = (np.outer(t, t) % S).astype(np.float64) * (2 * np.pi / S)
    tab = np.where((t < S // 2)[:, None], np.cos(tk), np.sin(tk))
    tab[S // 2, :] = 0.0
    tab = tab.astype(np.float32).astype(bf)
    dft = np.ascontiguousarray(tab[:, :1024].reshape(16, 128, 1024))
    inv = (10000.0 ** (-np.arange(8, dtype=np.float32) / 8.0)).astype(np.float32)
    row = (t // 64).astype(np.float32)
    col = (t % 64).astype(np.float32)
    rope = np.zeros((2, 32, NK), dtype=np.float32)
    rope[0, :, :C] = 1.0
    for i in range(32):
        pos = row if i < 16 else col
        a = i % 16
        ang = (pos * inv[a % 8]).astype(np.float32)
        rope[0, i, C:] = np.cos(ang)
        rope[1, i, C:] = np.sin(ang) * (-1.0 if a < 8 else 1.0)
    return ident, csc.astype(np.float32).astype(bf), dft, rope


_SWAP = np.array([(i // 16) * 16 + ((i % 16) + 8) % 16 for i in range(32)])


def _alts():
    bf = ml_dtypes.bfloat16
    k = np.arange(1024)
    altk = np.where(k % 2 == 0, 1.0, -1.0).astype(np.float32).astype(bf).reshape(1, 1024)
    p = np.arange(128)
    altp = np.stack([np.where(p % 2 == 0, 1.0, -1.0), np.ones(128)], axis=1).astype(np.float32).astype(bf)
    return {"altk": altk, "altp": altp}


def _prep(inp, b):
    f = lambda a: np.ascontiguousarray(a, dtype=np.float32)
    w_in = inp["w_in"][0]
    w_in_x = np.concatenate([w_in, w_in[:, 896:928][:, _SWAP]], axis=1)
    wqb = inp["w_q_b"][0]
    nr = [wqb[:, h * 96 + 64:h * 96 + 96] for h in range(8)]
    sw = [wqb[:, h * 96 + 64:h * 96 + 96][:, _SWAP] for h in range(8)]
    wqb_x = np.concatenate([wqb] + nr + sw, axis=1)
    wkvb = inp["w_kv_b"][0].reshape(128, 8, 128)
    wkv = np.concatenate([wkvb[:, :, :64].reshape(128, 512), wkvb[:, :, 64:].reshape(128, 512)], axis=1)
    g4 = np.stack([inp["g_pre_mix"][0], inp["g_post_mix"][0], inp["g_pre_ffn"][0], inp["g_post_ffn"][0]])
    return {
        "x": f(inp["x"][b]), "ctx": f(inp["ctx"][b]),
        "cvec": f(np.stack([inp["c"][b], inp["c_ctx"]])),
        "cvec_c": f(np.stack([inp["c"][b], inp["c_ctx"]]).reshape(2, 8, 128).transpose(2, 1, 0)),
        "g4_c": f(g4.reshape(4, 8, 128).transpose(2, 0, 1)),
        "b_ada_c": f(inp["b_ada"][0].reshape(48, 128).T),
        "gq_c": f(inp["g_q_a"][0].reshape(2, 128).T),
        "gkv_c": f(inp["g_kv_a"][0].reshape(1, 128).T),
        "w_ada": f(inp["w_ada"][0]), "b_ada": f(inp["b_ada"][0]), "g4": f(g4),
        "w_in": f(w_in_x), "g_q_a": f(inp["g_q_a"][0]), "w_q_b": f(wqb_x),
        "g_kv_a": f(inp["g_kv_a"][0]), "w_kv": f(wkv),
        "w_f": f(inp["w_fourier"][0].reshape(512, 128)), "w_out": f(inp["w_out"][0]),
        "w_gate": f(inp["w_gate"][0]), "w_up": f(inp["w_up"][0]), "w_down": f(inp["w_down"][0]),
    }


def kernel(**inputs):
    inp = {k: np.asarray(v) for k, v in inputs.items()}
    ident, csc, dft, rope = _consts()
    rope4 = np.ascontiguousarray(np.tile(rope, (1, 4, 1)))
    nc = build()
    in_maps = []
    for b in range(8):
        m = _prep(inp, b)
        m.update(ident=ident, csc=csc, dft=dft, rope=rope4)
        m.update(_alts())
        in_maps.append(m)
    res = run_bass_kernel_spmd(nc, in_maps, core_ids=list(range(8)))
    return np.stack([np.asarray(r["out"], dtype=np.float32) for r in res.results], axis=0)
```

```python
import os
from contextlib import ExitStack
import numpy as np
import ml_dtypes
import concourse.bass as bass
import concourse.mybir as mybir
from concourse.bass_utils import run_bass_kernel_spmd

F32 = mybir.dt.float32
BF16 = mybir.dt.bfloat16
AF = mybir.ActivationFunctionType
ALU = mybir.AluOpType

S = 2048
C = 256
D = 1024
NK = S + C
DFF = 2816
NFC = 22
EPS = 1e-6
SCALE = 96.0 ** -0.5
KB = 1024


class Op:
    __slots__ = ("eng", "fn", "deps", "chan", "idx", "flag", "val")

    def __init__(self, eng, fn, deps, chan):
        self.eng = eng
        self.fn = fn
        self.deps = deps
        self.chan = chan
        self.flag = False
        self.val = None


def _flat(deps):
    out = []
    for x in deps:
        if x is None:
            continue
        if isinstance(x, (list, tuple)):
            out.extend(_flat(x))
        else:
            out.append(x)
    return out


class Prog:
    ENGS = ("pe", "act", "dve", "pool", "sp")

    def __init__(self, nc):
        self.nc = nc
        self.ops = []
        self.nchan = 0

    def chan(self):
        self.nchan += 1
        return self.nchan - 1

    def add(self, eng, fn, deps=(), chan=None):
        op = Op(eng, fn, _flat(deps), chan)
        op.idx = len(self.ops)
        self.ops.append(op)
        return op

    def emit(self, stack):
        nc = self.nc
        ops = self.ops

        def skip(d, op):
            return d.chan is None and d.eng == "pe" and op.eng == "pe" and op.chan is None

        for op in ops:
            for d in op.deps:
                assert d.idx < op.idx
                if not skip(d, op):
                    d.flag = True
            if op.chan is not None:
                op.flag = True
        eng_cnt = {e: 0 for e in self.ENGS}
        chan_cnt = {}
        for op in ops:
            if op.chan is not None:
                chan_cnt[op.chan] = chan_cnt.get(op.chan, 0) + 16
                op.val = chan_cnt[op.chan]
            elif op.flag:
                eng_cnt[op.eng] += 1
                op.val = eng_cnt[op.eng]
        sems = {}
        for e in self.ENGS:
            sems[("e", e)] = stack.enter_context(nc.semaphore("s_" + e))
        for c in sorted(chan_cnt):
            sems[("c", c)] = stack.enter_context(nc.semaphore("c_%d" % c))
        per = {e: [o for o in ops if o.eng == e] for e in self.ENGS}
        block = stack.enter_context(nc.Block())

        def run(e, eng):
            seen = {}
            for op in per[e]:
                need = {}
                for d in op.deps:
                    if not d.flag or skip(d, op):
                        continue
                    k = ("c", d.chan) if d.chan is not None else ("e", d.eng)
                    if need.get(k, 0) < d.val:
                        need[k] = d.val
                for k, v in need.items():
                    if seen.get(k, 0) >= v:
                        continue
                    eng.wait_ge(sems[k], v)
                    seen[k] = v
                ins = op.fn(eng)
                if op.flag:
                    k = ("c", op.chan) if op.chan is not None else ("e", op.eng)
                    ins.then_inc(sems[k], 16 if op.chan is not None else 1)
            last = {}
            for op in per[e]:
                if op.chan is not None:
                    last[op.chan] = op.val
            for c, v in last.items():
                if seen.get(("c", c), 0) < v:
                    eng.wait_ge(sems[("c", c)], v)

        block.tensor(lambda eng: run("pe", eng))
        block.scalar(lambda eng: run("act", eng))
        block.vector(lambda eng: run("dve", eng))
        block.gpsimd(lambda eng: run("pool", eng))
        block.sync(lambda eng: run("sp", eng))


class _Stop(Exception):
    pass


class Buf:
    def __init__(self):
        self.w = []
        self.r = []

    def wdeps(self):
        return list(self.r) + list(self.w)

    def wrote(self, op, more=False):
        if more:
            self.w.append(op)
        else:
            self.w = [op]
            self.r = []
        return op

    def rdeps(self):
        return list(self.w)

    def read(self, op):
        self.r.append(op)
        return op


def build(dbg=None):
    nc = bass.Bass("TRN2", target_bir_lowering=False)

    def din(name, shape, dt=F32):
        return nc.dram_tensor(name, list(shape), dt, kind="ExternalInput").ap()

    x_d = din("x", [S, D])
    ctx_d = din("ctx", [C, D])
    cvec_d = din("cvec", [2, D])
    cvecc_d = din("cvec_c", [128, 8, 2])
    g4c_d = din("g4_c", [128, 4, 8])
    badac_d = din("b_ada_c", [128, 48])
    gqc_d = din("gq_c", [128, 2])
    gkvc_d = din("gkv_c", [128, 1])
    wada_d = din("w_ada", [D, 6 * D])
    bada_d = din("b_ada", [6 * D])
    g4_d = din("g4", [4, D])
    win_d = din("w_in", [D, 960])
    gq_d = din("g_q_a", [256])
    wqb_d = din("w_q_b", [256, 1280])
    gkv_d = din("g_kv_a", [128])
    wkv_d = din("w_kv", [128, 1024])
    wf_d = din("w_f", [512, 128])
    wout_d = din("w_out", [D, D])
    wg_d = din("w_gate", [D, DFF])
    wu_d = din("w_up", [D, DFF])
    wd_d = din("w_down", [DFF, D])
    ident_d = din("ident", [128, 128], BF16)
    csc_d = din("csc", [128, 256], BF16)
    dft_d = din("dft", [16, 128, 1024], BF16)
    altk_d = din("altk", [1, 1024], BF16)
    altp_d = din("altp", [128, 2], BF16)
    rope_d = din("rope", [2, 128, NK])
    out_d = nc.dram_tensor("out", [S, D], F32, kind="ExternalOutput").ap()
    x1_d = nc.dram_tensor("x1s", [S, D], F32, kind="Internal").ap()
    dbg_d = {}
    if dbg:
        for k, shp in dbg.items():
            dbg_d[k] = nc.dram_tensor("dbg_" + k, list(shp[0]), shp[1], kind="ExternalOutput").ap()

    P = Prog(nc)
    nm = [0]

    def T(shape, dt, off):
        nm[0] += 1
        return nc.alloc_sbuf_tensor_at("t%d" % nm[0], list(shape), dt, offset=off + 16640)

    R0, R1, R2, R3, R4, R5, R6, R7, R8 = [k * KB for k in (0, 36, 72, 104, 136, 152, 168, 184, 196)]
    hT = T([128, 8, S], BF16, R0)
    KT = T([128, 8, NK], BF16, R0)
    h2T = T([128, 8, S], BF16, R0)
    wring = [T([128, 8, 1024], BF16, R1 + i * 16 * KB) for i in range(2)]
    Vt = T([128, 18, 8, 128], BF16, R1)
    wdn = T([128, NFC, 1024], BF16, R1)
    xst = [T([128, 1024], F32, R2 + i * 4 * KB) for i in range(4)]
    xs = [T([128, 1024], BF16, R2 + 16 * KB + i * 2 * KB) for i in range(4)]
    junk = T([128, 1024], BF16, R2 + 24 * KB)
    tmpb = T([128, 1024], F32, R6 + 8 * KB)
    sqT = T([128, 2, 512], BF16, R2 + 26 * KB)
    rbc = T([128, 512], F32, R2 + 28 * KB)
    rt1 = T([128, 512], F32, R6 + 4 * KB)
    rt2 = T([128, 512], F32, R6 + 6 * KB)
    QT = T([128, 8, S], BF16, R2)
    PQ2 = T([128, 16, 512], BF16, R2 + 16 * KB)
    Zt = T([128, 4, 128], BF16, R7 + 11 * KB)
    dring2 = [T([128, 512], BF16, R7 + i * KB) for i in range(8)]
    ropet = T([128, 2, NK], F32, R3)
    qnT = T([128, 2, S], BF16, R3 + 18 * KB)
    kvnT = T([128, NK], BF16, R3 + 26 * KB)
    catT = T([128, 8, S], BF16, R3)
    wst = [[T([128, 8, 512], BF16, R3 + (2 * i + j) * 8 * KB) for j in range(2)] for i in range(2)]
    UT = T([128, 4, S], BF16, R4)
    p7x = [T([128, 1024], F32, R4 + i * 4 * KB) for i in range(4)]
    p7xs2 = [T([128, 1024], BF16, R2 + (14 + 2 * i) * KB) for i in range(2)]
    p7junk = T([128, 1024], BF16, R2 + 18 * KB)
    p7t = [T([128, 1024], F32, R2 + (20 + 4 * i) * KB) for i in range(2)]
    p8x = [T([128, 1024], F32, R4 + i * 4 * KB) for i in range(2)]
    p8o = [T([128, 1024], F32, R4 + 8 * KB + i * 4 * KB) for i in range(2)]
    win = T([128, 8, 960], BF16, R5)
    accS = [T([128, 1024], F32, R5 + i * 4 * KB) for i in range(2)]
    denS = [T([128, 1024], F32, R5 + 8 * KB + i * 4 * KB) for i in range(2)]
    RF = T([128, 4, S], BF16, R5)
    actT = T([128, NFC, 1024], BF16, R5)
    hTc = T([128, 8, C], BF16, R6)
    wout = T([128, 8, 1024], BF16, R6)
    wqb = T([128, 2, 1280], BF16, R7)
    wkv = T([128, 1024], BF16, R7 + 5 * KB)
    krT = T([128, NK], BF16, R7 + 7 * KB)
    PT = [T([128, 1024], BF16, R7 + i * 2 * KB) for i in range(3)]
    dring = [T([128, 1024], BF16, R7 + i * 2 * KB) for i in range(6)]
    G1 = T([128, 1024], F32, R8)
    G2 = T([128, 1024], F32, R8 + 4 * KB)
    o = R8 + 8 * KB
    wf = T([128, 4, 128], BF16, o); o += 1024
    csc = T([128, 256], BF16, o); o += 512
    ident = T([128, 128], BF16, o); o += 256
    ones_bf = T([128, 128], BF16, o); o += 256
    csil = T([128, 8, 2], F32, o); o += 64
    sil_bf = T([128, 8, 2], BF16, o); o += 32
    gcols = T([128, 4, 8], F32, o); o += 128
    bcols = T([128, 48], F32, o); o += 192
    modc = T([128, 6, 8, 2], F32, o); o += 384
    Acol = T([128, 3, 8], F32, o); o += 96
    gqc = T([128, 2], F32, o); o += 32
    gkvc = T([128, 1], F32, o); o += 32
    epst = T([128, 1], F32, o); o += 32
    ssq = T([128, 4], F32, o); o += 32
    rstd = T([128, 4], F32, o); o += 32
    ssq2 = T([128, 4], F32, o); o += 32
    rstd2 = T([128, 4], F32, o); o += 32
    ssq8 = T([128, 8], F32, o); o += 32
    rstd8 = T([128, 8], F32, o); o += 32
    assert o <= 212700, o
    sil_rep = T([128, 8, 128], BF16, R6 + 12 * KB)
    ones_f = T([128, 128], F32, R5 + 15 * KB)
    rq1 = T([128, 512], F32, R5)
    rq2 = T([128, 512], F32, R5 + 2 * KB)
    Rq4 = T([128, 2, S], BF16, R5 + 4 * KB)
    rqb = [[rq1, rq2], [T([128, 512], F32, R5 + 12 * KB), T([128, 512], F32, R5 + 14 * KB)]]
    p8junk = T([128, 1024], BF16, R2 + 12 * KB)
    sg = [T([128, 512], F32, R2 + 8 * KB + i * 2 * KB) for i in range(2)]

    psum = nc.alloc_psum_tensor("ps", [128, 8, 512], F32)
    bank = [Buf() for _ in range(8)]

    def pb(b, n=512, p0=0, p1=128):
        return psum[p0:p1, b, 0:n]

    def pb2(b, p0=0, p1=128):
        return psum[p0:p1, b:b + 2, :].rearrange("p a b -> p (a b)")

    tp_bf = psum[:, 0:4, :].rearrange("p a b -> p (a b)").bitcast(BF16).rearrange("p (k t) -> p k t", k=8)

    def mm(out, lhsT, rhs, start, stop, deps=(), tp=None):
        def f(e):
            if tp is None:
                return e.matmul(out, lhsT=lhsT, rhs=rhs, start=start, stop=stop)
            return e.matmul(out, lhsT=lhsT, rhs=rhs, start=start, stop=stop, tile_position=tp)
        return P.add("pe", f, deps)

    def tr(out, in_, deps=()):
        return P.add("pe", lambda e: e.transpose(out=out, in_=in_, identity=ident[:]), deps)

    def act(out, in_, func, deps=(), scale=1.0, bias=None, accum=None):
        def f(e):
            kw = {}
            if bias is not None:
                kw["bias"] = bias
            if accum is not None:
                kw["accum_out"] = accum
            return e.activation(out=out, in_=in_, func=func, scale=scale, **kw)
        return P.add("act", f, deps)

    def ts(eng, out, in0, s1, s2, op0, op1=None, deps=()):
        def f(e):
            if op1 is None:
                return e.tensor_scalar(out=out, in0=in0, scalar1=s1, scalar2=None, op0=op0)
            return e.tensor_scalar(out=out, in0=in0, scalar1=s1, scalar2=s2, op0=op0, op1=op1)
        return P.add(eng, f, deps)

    def tt(eng, out, in0, in1, op, deps=()):
        return P.add(eng, lambda e: e.tensor_tensor(out=out, in0=in0, in1=in1, op=op), deps)

    def stt(eng, out, in0, scalar, in1, op0, op1, deps=()):
        return P.add(eng, lambda e: e.scalar_tensor_tensor(out=out, in0=in0, scalar=scalar, in1=in1, op0=op0, op1=op1), deps)

    def cp(eng, out, in_, deps=()):
        if eng == "act":
            return act(out, in_, AF.Copy, deps)
        return P.add(eng, lambda e: e.tensor_copy(out=out, in_=in_), deps)

    def memset(eng, ap, v, deps=()):
        return P.add(eng, lambda e: e.memset(ap, v), deps)

    def dma(q, out, in_, deps=(), chan=None, slow=False):
        if chan is None:
            chan = P.chan()
        def f(e):
            if slow:
                return e.dma_start(out=out, in_=in_, allow_slow_non_contiguous=True)
            return e.dma_start(out=out, in_=in_)
        return P.add(q, f, deps, chan)

    def tap(name, ap, deps):
        if dbg and name in dbg_d:
            dma("sp", dbg_d[name], ap, deps)

    def rstd_ops(dst, src, n, deps):
        a = act(dst, src, AF.Ln, deps, scale=1.0 / n, bias=epst[:])
        return act(dst, dst, AF.Exp, [a], scale=-0.5)

    STOP = int(os.environ.get('MK_STOP', '99'))
    SUB = int(os.environ.get('MK_SUB', '0'))
    try:
        ld_cvec = dma("sp", csil[:], cvecc_d)
        ld_ident = dma("sp", ident[:], ident_d)
        m_eps = memset("dve", epst[:], EPS)
        m_ob = memset("dve", ones_bf[:], 1.0)
        m_of = memset("dve", ones_f[:], 1.0)
        a_sil = act(sil_bf[:], csil[:], AF.Silu, [ld_cvec])
        srep = []
        for kc in range(8):
            srep.append(ts("dve", sil_rep[:, kc, :], ones_f[:], sil_bf[:, kc, 0:1], None, ALU.mult, deps=[a_sil, m_of]))

        wada_v = wada_d.rearrange("(kc p) n -> p kc n", p=128)
        ring_buf = [Buf(), Buf()]
        ring_chan = [P.chan(), P.chan()]
        order = [1, 0, 4, 3, 2, 5]
        ada_ld = {}

        def issue_ada_load(i):
            j = order[i]
            s = i % 2
            ada_ld[j] = ring_buf[s].wrote(dma("pool", wring[s][:], wada_v[:, :, j * 1024:(j + 1) * 1024],
                                               [ring_buf[s].wdeps(), w_in_ld if i >= 2 else None], chan=ring_chan[s]))

        tmpb_buf = Buf()

        def ada_cols(i):
            j = order[i]
            s = i % 2
            cv = psum[:, 7, 0:16].rearrange("p (n r) -> p n r", r=2)
            last = None
            w0 = bank[7].wdeps()
            for nn in range(8):
                for kc in range(8):
                    last = mm(cv[:, nn, :], wring[s][:, kc, nn * 128:(nn + 1) * 128], sil_bf[:, kc, :], kc == 0, kc == 7,
                              [ada_ld[j], a_sil, w0])
            bank[7].wrote(last)
            ring_buf[s].read(last)
            ev = []
            for r in range(2):
                ev.append(bank[7].read(tt("dve", modc[:, j, :, r], cv[:, :, r], bcols[:, j * 8:(j + 1) * 8], ALU.add,
                                          [last, ld_bc])))
            return ev

        def ada_gate(i, G, grow):
            j = order[i]
            s = i % 2
            l1 = tmpb_buf.wrote(dma("sp", tmpb[:], bada_d[j * 1024:(j + 1) * 1024].partition_broadcast(128), tmpb_buf.wdeps()))
            l2 = gpre[grow]
            res = []
            for nh in range(2):
                b = 5 + nh
                w0 = bank[b].wdeps()
                last = None
                for kc in range(8):
                    last = mm(pb(b), sil_rep[:, kc, :], wring[s][:, kc, nh * 512:(nh + 1) * 512], kc == 0, kc == 7,
                              [ada_ld[j], srep, w0])
                bank[b].wrote(last)
                ring_buf[s].read(last)
                sl = slice(nh * 512, (nh + 1) * 512)
                a = bank[b].read(tt("dve", tmpb[:, sl], pb(b), tmpb[:, sl], ALU.add, [last, l1]))
                res.append(tmpb_buf.read(tt("dve", G[:, sl], tmpb[:, sl], G[:, sl], ALU.mult, [a, l2])))
            return res

        xst_buf = [Buf() for _ in range(4)]
        xst_chan = [P.chan() for _ in range(4)]
        xs_buf = [Buf() for _ in range(4)]
        junk_buf = Buf()
        ssq_buf = [Buf(), Buf()]
        hT_done = []
        tile_ctr = [0]

        def norm1(src_d, ntile, par):
            sq_ops = []
            ms = []
            c0 = par * 4
            for j in range(ntile):
                s = j
                ld = xst_buf[s].wrote(dma("sp", xst[s][:], src_d[j * 128:(j + 1) * 128, :], xst_buf[s].wdeps(), chan=xst_chan[s]))
                a = act(junk[:], xst[s][:], AF.Square, [ld, junk_buf.wdeps(), ssq_buf[par].wdeps() if j == 0 else None], accum=ssq8[:, c0 + j:c0 + j + 1])
                junk_buf.wrote(a)
                xst_buf[s].read(a)
                sq_ops.append((a, ld))
            ssq_buf[par].wrote(sq_ops[-1][0])
            r = rstd_ops(rstd8[:, c0:c0 + ntile], ssq8[:, c0:c0 + ntile], D, [[q[0] for q in sq_ops], m_eps])
            for j in range(ntile):
                a, ld = sq_ops[j]
                m = ts("dve", xs[j][:], xst[j][:], rstd8[:, c0 + j:c0 + j + 1], None, ALU.mult, deps=[r, ld, xs_buf[j].wdeps()])
                xs_buf[j].wrote(m)
                xst_buf[j].read(m)
                ssq_buf[par].read(m)
                ms.append(m)
            return ms

        def norm2(ms, ntile, dstT, tok0, Ac, shc_j, shc_r, extra_deps, split=False):
            tps = []
            w0 = [bank[b].wdeps() for b in range(4)]
            for j in range(ntile):
                for kc in range(8):
                    t = tr(tp_bf[:, kc, j * 128:(j + 1) * 128], xs[j][:, kc * 128:(kc + 1) * 128], [ms[j], ld_ident, w0[kc // 2]])
                    tps.append(t)
                xs_buf[j].read(tps[-1])
            for b in range(4):
                bank[b].wrote(tps[-1])
            n = ntile * 128

            def evac(kcs):
                evs = []
                for kc in kcs:
                    e = act(dstT[:, kc, tok0:tok0 + n], tp_bf[:, kc, 0:n], AF.Identity, [tps[-1], extra_deps],
                            scale=Ac[:, kc:kc + 1], bias=modc[:, shc_j, kc, shc_r:shc_r + 1])
                    bank[kc // 2].read(e)
                    evs.append(e)
                return evs
            if split:
                return evac
            return evac(range(8))

        def kv_norm(b_kv, n, tok0, deps_mm):
            a = bank[b_kv].read(act(sqT[:, 0, 0:n], pb(b_kv, n), AF.Square, [deps_mm]))
            m = bank[7].wrote(mm(pb(7, n), ones_bf[:], sqT[:, 0, 0:n], True, True, [a, m_ob, bank[7].wdeps()]))
            r1 = bank[7].read(act(rbc[:, 0:n], pb(7, n), AF.Ln, [m, m_eps], scale=1.0 / 128, bias=epst[:]))
            r2 = act(rbc[:, 0:n], rbc[:, 0:n], AF.Exp, [r1], scale=-0.5)
            return bank[b_kv].read(stt("dve", kvnT[:, tok0:tok0 + n], pb(b_kv, n), gkvc[:, 0:1], rbc[:, 0:n], ALU.mult, ALU.mult,
                                       [r2, ld_gkv, deps_mm]))

        ms_g = {0: norm1(x_d[0:512, :], 4, 0)}
        ld_g4 = dma("sp", gcols[:], g4c_d)
        gpre = {}
        ld_bc = dma("sp", bcols[:], badac_d)
        ld_gq = dma("sp", gqc[:], gqc_d)
        ld_gkv = dma("sp", gkvc[:], gkvc_d)
        gpre = {1: dma("sp", G1[:], g4_d[1, :].partition_broadcast(128)), 3: dma("sp", G2[:], g4_d[3, :].partition_broadcast(128))}
        issue_ada_load(0)
        issue_ada_load(1)
        w_in_ld = dma("pool", win[:], win_d.rearrange("(kc p) n -> p kc n", p=128))
        ld_rope = dma("pool", ropet[:], rope_d.rearrange("c i t -> i c t"), [w_in_ld])
        ev_sc = ada_cols(0)
        issue_ada_load(2)
        ev_sh = ada_cols(1)
        issue_ada_load(3)
        A1ops = []
        for r in range(2):
            A1ops.append(stt("dve", Acol[:, r, :], modc[:, 1, :, r], 1.0, gcols[:, 0, :], ALU.add, ALU.mult, [ev_sc, ld_g4]))
        mod_ready = [A1ops, ev_sh]

        ld_wqb = dma("pool", wqb[:], wqb_d.rearrange("(c p) n -> p c n", p=128), [w_in_ld])
        ld_wkv = dma("pool", wkv[:], wkv_d, [w_in_ld])
        ld_csc = dma("pool", csc[:], csc_d, [w_in_ld])

        UT_ops = []
        qn_ops = []
        kr_ops = []
        kvn_ops = []
        flip = [0]

        def evac_copy(out, in_, deps):
            flip[0] ^= 1
            return cp("act" if flip[0] else "dve", out, in_, deps)

        sq_buf = Buf()
        rbc_buf = Buf()
        ev_g = {0: norm2(ms_g[0], 4, hT, 0, Acol[:, 0, :], 0, 0, mod_ready)}
        ms_c = None
        for gi in range(4):
            t0 = gi * 512
            ev = ev_g[gi]
            if gi == 2:
                ev_scf = ada_cols(2)
                issue_ada_load(4)
                ev_shf = ada_cols(3)
                issue_ada_load(5)
                A2op = stt("dve", Acol[:, 2, :], modc[:, 4, :, 0], 1.0, gcols[:, 2, :], ALU.add, ALU.mult, [ev_scf, ld_g4])
            if gi + 1 < 4:
                ms_g[gi + 1] = norm1(x_d[t0 + 512:t0 + 1024, :], 4, (gi + 1) % 2)
            else:
                ms_c = norm1(ctx_d, 2, 0)
            for g in range(4):
                b = 4 + (g % 2)
                w0 = bank[b].wdeps()
                for kc in range(8):
                    last = mm(pb(b), win[:, kc, g * 128:(g + 1) * 128], hT[:, kc, t0:t0 + 512], kc == 0, kc == 7, [ev, w_in_ld, w0])
                bank[b].wrote(last)
                UT_ops.append(bank[b].read(evac_copy(UT[:, g, t0:t0 + 512], pb(b), [last])))
            if gi + 1 < 4:
                evf = norm2(ms_g[gi + 1], 4, hT, t0 + 512, Acol[:, 0, :], 0, 0, mod_ready, split=True)
                ev_g[gi + 1] = []
                ev_next = ev_g[gi + 1]
            else:
                evf = norm2(ms_c, 2, hTc, 0, Acol[:, 1, :], 0, 1, mod_ready, split=True)
                ev_c = []
                ev_next = ev_c
            qmm = []
            for c in range(2):
                b = 5 + c
                w0 = bank[b].wdeps()
                for kc in range(8):
                    last = mm(pb(b), win[:, kc, 512 + c * 128:512 + (c + 1) * 128], hT[:, kc, t0:t0 + 512], kc == 0, kc == 7, [ev, w_in_ld, w0])
                bank[b].wrote(last)
                qmm.append(last)
            sqs = []
            for c in range(2):
                sqs.append(bank[5 + c].read(act(sqT[:, c, :], pb(5 + c), AF.Square, [qmm[c], sq_buf.wdeps()])))
            ev_next += evf(range(0, 4))
            w0 = bank[7].wdeps()
            for c in range(2):
                last = mm(pb(7), ones_bf[:], sqT[:, c, :], c == 0, c == 1, [sqs, m_ob, w0])
            bank[7].wrote(last)
            sq_buf.wrote(sqs[-1]); sq_buf.read(last)
            r1 = bank[7].read(act(rbc[:], pb(7), AF.Ln, [last, m_eps, rbc_buf.wdeps()], scale=1.0 / 256, bias=epst[:]))
            r2 = rbc_buf.wrote(act(rbc[:], rbc[:], AF.Exp, [r1], scale=-0.5))
            for c in range(2):
                o_ = stt("dve", qnT[:, c, t0:t0 + 512], pb(5 + c), gqc[:, c:c + 1], rbc[:], ALU.mult, ALU.mult, [r2, ld_gq, qmm[c]])
                bank[5 + c].read(o_)
                rbc_buf.read(o_)
                qn_ops.append(o_)
            w0 = bank[4].wdeps()
            for kc in range(8):
                last = mm(pb(4), win[:, kc, 768:896], hT[:, kc, t0:t0 + 512], kc == 0, kc == 7, [ev, w_in_ld, w0])
            bank[4].wrote(last)
            a = bank[4].read(act(sqT[:, 0, :], pb(4), AF.Square, [last, sq_buf.wdeps()]))
            sq_buf.wrote(a)
            ev_next += evf(range(4, 8))
            m = bank[7].wrote(mm(pb(7), ones_bf[:], sqT[:, 0, :], True, True, [a, m_ob, bank[7].wdeps()]))
            sq_buf.read(m)
            r1 = bank[7].read(act(rbc[:], pb(7), AF.Ln, [m, m_eps, rbc_buf.wdeps()], scale=1.0 / 128, bias=epst[:]))
            r2 = rbc_buf.wrote(act(rbc[:], rbc[:], AF.Exp, [r1], scale=-0.5))
            o_ = stt("dve", kvnT[:, C + t0:C + t0 + 512], pb(4), gkvc[:, 0:1], rbc[:], ALU.mult, ALU.mult, [r2, ld_gkv, last])
            bank[4].read(o_); rbc_buf.read(o_)
            kvn_ops.append(o_)
            krm = []
            for c in range(2):
                b = 5 + c
                w0 = bank[b].wdeps()
                for kc in range(8):
                    last = mm(pb(b, 512, 64, 96), win[:, kc, 896 + c * 32:928 + c * 32], hT[:, kc, t0:t0 + 512], kc == 0, kc == 7,
                              [ev, w_in_ld, w0], tp=(0, 64))
                bank[b].wrote(last)
                krm.append(last)
            hT_done.append(last)
            c1 = bank[5].read(cp("dve", rt1[64:96, :], pb(5, 512, 64, 96), [krm[0], kr_ops[-1:]]))
            c2 = bank[6].read(cp("act", rt2[64:96, :], pb(6, 512, 64, 96), [krm[1], kr_ops[-1:]]))
            k1 = tt("pool", rt1[64:96, :], rt1[64:96, :], ropet[64:96, 0, C + t0:C + t0 + 512], ALU.mult, [c1, ld_rope])
            k2 = tt("pool", rt2[64:96, :], rt2[64:96, :], ropet[64:96, 1, C + t0:C + t0 + 512], ALU.mult, [c2, ld_rope, k1])
            kr_ops.append(tt("pool", krT[64:96, C + t0:C + t0 + 512], rt1[64:96, :], rt2[64:96, :], ALU.add, [k1, k2]))

        last = None
        w0 = bank[4].wdeps()
        for kc in range(8):
            last = mm(pb(4, C), win[:, kc, 768:896], hTc[:, kc, :], kc == 0, kc == 7, [ev_c, w_in_ld, w0])
        bank[4].wrote(last)
        a = bank[4].read(act(sqT[:, 0, 0:C], pb(4, C), AF.Square, [last, sq_buf.wdeps()]))
        sq_buf.wrote(a)
        m = bank[7].wrote(mm(pb(7, C), ones_bf[:], sqT[:, 0, 0:C], True, True, [a, m_ob, bank[7].wdeps()]))
        sq_buf.read(m)
        r1 = bank[7].read(act(rbc[:, 0:C], pb(7, C), AF.Ln, [m, m_eps, rbc_buf.wdeps()], scale=1.0 / 128, bias=epst[:]))
        r2 = rbc_buf.wrote(act(rbc[:, 0:C], rbc[:, 0:C], AF.Exp, [r1], scale=-0.5))
        o_ = stt("dve", kvnT[:, 0:C], pb(4, C), gkvc[:, 0:1], rbc[:, 0:C], ALU.mult, ALU.mult, [r2, ld_gkv, last])
        bank[4].read(o_); rbc_buf.read(o_)
        kvn_ops.append(o_)
        w0 = bank[5].wdeps()
        for kc in range(8):
            last = mm(pb(5, C, 64, 96), win[:, kc, 896:928], hTc[:, kc, :], kc == 0, kc == 7, [ev_c, w_in_ld, w0], tp=(0, 64))
        bank[5].wrote(last)
        kr_ops.append(bank[5].read(cp("dve", krT[64:96, 0:C], pb(5, C, 64, 96), [last])))
        hTc_done = last
        tap("hT", hT[:], UT_ops)
        tap("modc", modc[:], [UT_ops, ev_scf, ev_shf])
        tap("Acol", Acol[:], [UT_ops, A2op])
        tap("UT", UT[:], UT_ops)
        tap("qnT", qnT[:], qn_ops)
        tap("kvnT", kvnT[:], kvn_ops)
        tap("krT", krT[64:96, :], kr_ops)

        if STOP == 4:
            raise _Stop
        ph3_done = [hT_done, hTc_done, kvn_ops, kr_ops, qn_ops[-1]]
        KT_w = []
        rot = [0]

        def nextbank(n=6):
            b = rot[0] % n
            rot[0] += 1
            return b

        QT_w = []
        V_w = []
        rt_buf = [Buf(), Buf()]
        jobsD = []
        jobsA = []
        rq_state = {0: [], 1: []}

        def job_qrope(g, tq):
            def f():
                t0 = tq * 512
                bs_ = []
                for v in range(2):
                    b = nextbank()
                    w0 = bank[b].wdeps()
                    for rc in range(2):
                        last = mm(pb(b), wqb[:, rc, 768 + v * 256 + g * 128:768 + v * 256 + (g + 1) * 128], qnT[:, rc, t0:t0 + 512],
                                  rc == 0, rc == 1, [qn_ops, ld_wqb, w0])
                    bank[b].wrote(last)
                    bs_.append((b, last))
                par_ = (g * 4 + tq) % 2
                ra, rb_ = rqb[par_]
                k1 = bank[bs_[0][0]].read(stt("dve", ra[:], pb(bs_[0][0]), SCALE, ropet[:, 0, C + t0:C + t0 + 512], ALU.mult, ALU.mult,
                                              [bs_[0][1], rt_buf[par_].wdeps(), ld_rope, ph3_done]))
                k2 = bank[bs_[1][0]].read(stt("dve", rb_[:], pb(bs_[1][0]), SCALE, ropet[:, 1, C + t0:C + t0 + 512], ALU.mult, ALU.mult,
                                              [bs_[1][1], rt_buf[par_].wdeps(), ld_rope, ph3_done]))
                rt_buf[par_].wrote(k1); rt_buf[par_].wrote(k2, more=True)
                o_ = tt("pool", Rq4[:, g, t0:t0 + 512], ra[:], rb_[:], ALU.add, [k1, k2])
                rt_buf[par_].read(o_)
                rq_state[g].append(o_)
                if tq == 3:
                    qch = P.chan()
                    for hl in range(4):
                        QT_w.append(dma("sp", QT[64:96, 4 * g + hl, :], Rq4[hl * 32:(hl + 1) * 32, g, :],
                                        [rq_state[g], ph3_done, kr_ops, kvn_ops], chan=qch))
            return f

        def job_knope(h, c0):
            def f():
                n = min(512, NK - c0)
                b = nextbank()
                m = bank[b].wrote(mm(pb(b, n, 0, 64), wkv[:, h * 64:(h + 1) * 64], kvnT[:, c0:c0 + n], True, True,
                                     [kvn_ops, ld_wkv, bank[b].wdeps()]))
                KT_w.append(bank[b].read(cp("dve", KT[0:64, h, c0:c0 + n], pb(b, n, 0, 64), [m, ph3_done])))
                if c0 == 0:
                    KT_w.append(cp("dve", KT[64:96, h, :], krT[64:96, :], [kr_ops, ph3_done]))
            return f

        def job_v(tkt):
            def f():
                b = nextbank()
                m = bank[b].wrote(mm(pb(b), kvnT[:, tkt * 128:(tkt + 1) * 128], wkv[:, 512:1024], True, True, [kvn_ops, ld_wkv, bank[b].wdeps()]))
                pv = pb(b).rearrange("p (h d) -> p h d", h=8)
                V_w.append(bank[b].read(cp("act", Vt[:, tkt, 0:8:2, 0:64], pv[:, 0:8:2, :], [m, ph3v])))
                V_w.append(bank[b].read(cp("act", Vt[:, tkt, 1:8:2, 64:128], pv[:, 1:8:2, :], [m, ph3v])))
            return f

        def job_qnope(h, tq, eng):
            def f():
                t0 = tq * 512
                b = nextbank()
                w0 = bank[b].wdeps()
                for rc in range(2):
                    last = mm(pb(b, 512, 0, 64), wqb[:, rc, h * 96:h * 96 + 64], qnT[:, rc, t0:t0 + 512], rc == 0, rc == 1, [qn_ops, ld_wqb, w0])
                ma = bank[b].wrote(last)
                if eng == "act":
                    e = act(QT[0:64, h, t0:t0 + 512], pb(b, 512, 0, 64), AF.Copy, [ma, ph3_done, kr_ops, kvn_ops], scale=SCALE)
                else:
                    e = ts("dve", QT[0:64, h, t0:t0 + 512], pb(b, 512, 0, 64), SCALE, None, ALU.mult, deps=[ma, ph3_done, kr_ops, kvn_ops])
                QT_w.append(bank[b].read(e))
            return f

        g1_ops = ada_gate(4, G1, 1)
        g2_ops = ada_gate(5, G2, 3)
        ring_last = [ring_buf[0].wdeps(), ring_buf[1].wdeps(), g1_ops, g2_ops]
        ph3v = [ph3_done, ring_last]
        ms_v = []
        for g in range(2):
            for tq in range(4):
                jobsD.append(job_qrope(g, tq))
        for h in range(8):
            for c0 in range(0, NK, 512):
                jobsD.append(job_knope(h, c0))
        for tkt in range(18):
            jobsA.append(job_v(tkt))
        qi = 0
        for h in range(8):
            for tq in range(4):
                qi += 1
                if qi % 8 == 0:
                    jobsD.append(job_qnope(h, tq, "dve"))
                else:
                    jobsA.append(job_qnope(h, tq, "act"))
        ia = ib = 0
        while ia < len(jobsD) or ib < len(jobsA):
            if ia < len(jobsD):
                jobsD[ia](); ia += 1
                if ia == 8:
                    ms_v += [memset("pool", Vt[:, :, 0:8:2, 64:128], 1.0, [ph3v]),
                             memset("pool", Vt[:, :, 1:8:2, 0:64], 1.0, [ph3v])]
            if ib < len(jobsA):
                jobsA[ib](); ib += 1
        V_w.append(ms_v)
        tap("KT", KT[0:96, :, :], KT_w)
        tap("QT", QT[0:96, :, :], QT_w)
        tap("Vt", Vt[:], V_w)
        ph4_done = [KT_w, V_w, QT_w]

        if STOP == 5:
            raise _Stop
        ld_wout = dma("pool", wout[:], wout_d.rearrange("(kc p) n -> p kc n", p=128), [ph3_done, kr_ops, ring_last])

        PT_buf = [Buf() for _ in range(3)]
        accS_buf = [Buf(), Buf()]
        denS_buf = [Buf(), Buf()]
        den_chan = [P.chan(), P.chan()]
        cat_attn = []
        its = [(h, half, tkt) for h in range(8) for half in range(2) for tkt in range(18)]
        nit = len(its)
        exp_ops = {}
        dtab = T([128, 16, 1024], BF16, R0)
        ld_tab_early = []

        def emit_S(i):
            h, half, tkt = its[i]
            tq0 = half * 1024
            sb_ = (i % 2) * 2
            w0 = [bank[sb_].wdeps(), bank[sb_ + 1].wdeps()]
            for j in range(2):
                m = mm(pb(sb_ + j), KT[0:96, h, tkt * 128:(tkt + 1) * 128], QT[0:96, h, tq0 + j * 512:tq0 + (j + 1) * 512], True, True,
                       [ph4_done if i < 2 else None, w0[j]])
                bank[sb_ + j].wrote(m)
            s = i % 3
            e = act(PT[s][:], pb2(sb_), AF.Exp, [m, PT_buf[s].wdeps()])
            bank[sb_].read(e); bank[sb_ + 1].read(e)
            PT_buf[s].wrote(e)
            exp_ops[i] = e
            exp_ops["lastS"] = m
            if half == 1 and tkt == 17 and h in (1, 3, 5):
                c_ = h // 2
                ld_tab_early.append(dma("sp", dtab[:, 4 * c_:4 * c_ + 4, :], dft_d[4 * c_:4 * c_ + 4].rearrange("t p k -> p t k"), [m]))

        def emit_PV(i):
            h, half, tkt = its[i]
            tq0 = half * 1024
            ab = 4 + 2 * ((h * 2 + half) % 2)
            s = i % 3
            e = exp_ops.pop(i)
            if tkt == 0:
                emit_PV.wacc = [bank[ab].wdeps(), bank[ab + 1].wdeps()]
            for j in range(2):
                lastpv = mm(pb(ab + j), Vt[:, tkt, h, :], PT[s][:, j * 512:(j + 1) * 512], tkt == 0, tkt == 17,
                            [e, emit_PV.wacc[j] if tkt == 0 else None])
            PT_buf[s].read(lastpv)
            if tkt == 17:
                par = h % 2
                nr0 = par * 64
                dr0 = (1 - par) * 64
                bank[ab].wrote(lastpv); bank[ab + 1].wrote(lastpv)
                q = (h * 2 + half) % 2
                ev = cp("dve", accS[q][:], pb2(ab), [lastpv, accS_buf[q].wdeps(), ph4_done])
                bank[ab].read(ev); bank[ab + 1].read(ev)
                accS_buf[q].wrote(ev)
                d = dma("sp", denS[q][nr0:nr0 + 64, :], accS[q][dr0:dr0 + 64, :], [ev, denS_buf[q].wdeps()], chan=den_chan[q])
                accS_buf[q].read(d)
                denS_buf[q].wrote(d)
                rc_ = P.add("dve", lambda e_, q=q, nr0=nr0: e_.reciprocal(out=denS[q][nr0:nr0 + 64, :], in_=denS[q][nr0:nr0 + 64, :]), [d])
                o_ = tt("pool", catT[nr0:nr0 + 64, 4 + h // 2, tq0:tq0 + 1024], accS[q][nr0:nr0 + 64, :], denS[q][nr0:nr0 + 64, :], ALU.mult,
                        [rc_, ph4_done])
                accS_buf[q].read(o_); denS_buf[q].read(o_)
                cat_attn.append(o_)
            return lastpv

        def emit_fold():
            up_rev = UT[:, :, 2047:1024:-1]
            lo = UT[:, :, 1:1024]
            f1 = tt("dve", up_rev, lo, up_rev, ALU.subtract, [UT_ops])
            f2 = stt("dve", lo, lo, 2.0, up_rev, ALU.mult, ALU.subtract, [f1])
            zz = memset("pool", Zt[:], 0.0, [ph4_done])
            z1 = cp("dve", Zt[:, :, 0:1], UT[:, :, 1024:1025], [zz, UT_ops])
            z2 = memset("dve", UT[:, :, 1024:1025], 0.0, [z1])
            fold_ops = [f1, f2, z1, z2]
            return fold_ops

        emit_S(0)
        emit_S(1)
        for i in range(nit):
            if i + 2 < nit:
                emit_S(i + 2)
            lastpv = emit_PV(i)
            if i == 8:
                fold_ops = emit_fold()
        tap("attnT", catT[:, 4:8, :], cat_attn)
        ph5_done = [cat_attn, lastpv]

        if STOP == 6:
            raise _Stop
        altk = T([128, 1024], BF16, R0 + 32 * KB)
        altp = T([128, 2], BF16, R0 + 34 * KB)
        kt_dead = [exp_ops["lastS"]]
        ld_tab = ld_tab_early + [dma("sp", dtab[:, 12:16, :], dft_d[12:16].rearrange("t p k -> p t k"), kt_dead)]
        assert len(ld_tab) == 4
        ld_altk = dma("sp", altk[0:1, :], altk_d, kt_dead)
        ld_altp = dma("sp", altp[:], altp_d, kt_dead)
        ld_wf = dma("pool", wf[:], wf_d.rearrange("(g m) d -> m g d", m=128))
        PQ_w = []
        for tt_ in range(16):
            b = nextbank(4)
            w0 = bank[b].wdeps()
            sinp = tt_ >= 8
            for g in range(4):
                o_ap = psum[:, b, g * 128:(g + 1) * 128]
                last = mm(o_ap, UT[:, g, tt_ * 128:(tt_ + 1) * 128], csc[:, 128:256] if sinp else csc[:, 0:128], True, tt_ != 8,
                          [fold_ops, ld_csc, w0])
                if tt_ == 8:
                    last = mm(o_ap, Zt[:, g, :], csc[:, 0:128], False, True, [fold_ops])
            bank[b].wrote(last)
            e = cp("act", PQ2[:, tt_, :], pb(b), [last, exp_ops["lastS"]])
            bank[b].read(e)
            PQ_w.append(e)
        NR = 8
        dr_buf = [Buf() for _ in range(NR)]
        dr_chan = [P.chan() for _ in range(NR)]
        RF_w = []
        di = 0
        tmpS = [T([128, 512], F32, R4), T([128, 512], F32, R4 + 2 * KB)]
        pq_last = last
        tmpS_buf = [Buf(), Buf()]
        bn = 0
        wn = bank[bn].wdeps()
        for g in range(4):
            for tt_ in range(8):
                last = mm(psum[:, bn, g:g + 1], PQ2[:, tt_, g * 128:(g + 1) * 128], altp[:, 0:1], tt_ == 0, False, [ld_altp, wn, PQ_w])
            last = mm(psum[:, bn, g:g + 1], PQ2[0:1, 8, g * 128:(g + 1) * 128], altp[0:1, 1:2], False, True, [ld_altp])
        bank[bn].wrote(last)
        RF_w.append(bank[bn].read(cp("dve", RF[:, :, 1024:1025], psum[:, bn, 0:4].rearrange("p (g o) -> p g o", o=1), [last, ph5_done])))
        units = [(kq, gp) for kq in range(2) for gp in range(2)]
        for ui, (kq, gp) in enumerate(units):
            ab = 4 if ui % 2 == 0 else 0
            w0 = [bank[ab + i].wdeps() for i in range(4)]
            for tt_ in range(16):
                for gl in range(2):
                    g = gp * 2 + gl
                    b = ab + gl if tt_ < 8 else ab + 2 + gl
                    first = tt_ in (0, 8)
                    last = mm(pb(b), PQ2[:, tt_, g * 128:(g + 1) * 128], dtab[:, tt_, kq * 512:(kq + 1) * 512], first, tt_ == 15,
                              [ld_tab[tt_ // 4], PQ_w if ui < 2 else None, w0[b - ab] if first else None])
            lastS_ = last
            for gl in range(2):
                g = gp * 2 + gl
                last = mm(pb(ab + gl), PQ2[0:1, 8, g * 128:(g + 1) * 128], altk[0:1, kq * 512:(kq + 1) * 512], False, True, [ld_altk])
            for i in range(4):
                bank[ab + i].wrote(last)
            k0 = kq * 512
            for gl in range(2):
                g = gp * 2 + gl
                bc, bs = ab + gl, ab + 2 + gl
                q = (ui * 2 + gl) % 2
                cS = bank[bs].read(act(tmpS[q][:], pb(bs), AF.Copy, [lastS_, tmpS_buf[q].wdeps(), pq_last]))
                tmpS_buf[q].wrote(cS)
                e1 = bank[bc].read(tt("dve", RF[:, g, k0:k0 + 512], pb(bc), tmpS[q][:], ALU.add, [last, cS, ph5_done]))
                if kq == 0:
                    e2 = bank[bc].read(tt("dve", RF[:, g, 2047:1536:-1], psum[:, bc, 1:512], tmpS[q][:, 1:512], ALU.subtract, [last, cS, ph5_done]))
                else:
                    e2 = bank[bc].read(tt("dve", RF[:, g, 1536:1024:-1], pb(bc), tmpS[q][:], ALU.subtract, [last, cS, ph5_done]))
                tmpS_buf[q].read(e1); tmpS_buf[q].read(e2)
                RF_w += [e1, e2]
        tap("RF", RF[:], RF_w)
        cat_f = []
        for g in range(4):
            for kq in range(4):
                b = nextbank(4)
                m = bank[b].wrote(mm(pb(b), wf[:, g, :], RF[:, g, kq * 512:(kq + 1) * 512], True, True, [RF_w, ld_wf, bank[b].wdeps()]))
                cat_f.append(bank[b].read(evac_copy(catT[:, g, kq * 512:(kq + 1) * 512], pb(b), [m, ph4_done])))
        tap("fourT", catT[:, 0:4, :], cat_f)
        ph6_done = [cat_f, last]

        if STOP == 7:
            raise _Stop

        wgv = wg_d.rearrange("(kc p) n -> p kc n", p=128)
        wuv = wu_d.rearrange("(kc p) n -> p kc n", p=128)
        wst3 = [wst[0], wst[1], [T([128, 8, 512], BF16, R5), T([128, 8, 512], BF16, R5 + 8 * KB)]]
        wst_buf = [Buf(), Buf(), Buf()]
        wst_chan = [[P.chan(), P.chan()] for _ in range(3)]
        blk_ld = {}

        def issue_blk(k, deps):
            fb = k % 6
            ncol = 512 if fb < 5 else 256
            s_ = 2 if k == 0 else k % 2
            wd0 = wst_buf[s_].wdeps()
            lg = dma("pool", wst3[s_][0][:, :, 0:ncol], wgv[:, :, fb * 512:fb * 512 + ncol], [wd0, deps], chan=wst_chan[s_][0])
            lu = dma("pool", wst3[s_][1][:, :, 0:ncol], wuv[:, :, fb * 512:fb * 512 + ncol], [wd0, deps], chan=wst_chan[s_][1])
            wst_buf[s_].wrote(lg); wst_buf[s_].wrote(lu, more=True)
            blk_ld[k] = (lg, lu)

        issue_blk(0, [ph6_done, ph5_done])

        p7x_buf = [Buf() for _ in range(4)]
        p7x_chan = [P.chan() for _ in range(4)]
        p7t_buf = [Buf(), Buf()]
        p7xs_buf = [Buf(), Buf()]
        p7j_buf = Buf()
        st_chan = [P.chan() for _ in range(4)]
        x1_st = []
        h2_w = []
        tp7v = [psum[:, 6:8, :].rearrange("p a b -> p (a b)").bitcast(BF16).rearrange("p (k t) -> p k t", k=8)]
        st7 = {}

        def p7A(t):
            s_ = t % 4
            ld = p7x_buf[s_].wrote(dma("sp", p7x[s_][:], x_d[t * 128:(t + 1) * 128, :], [p7x_buf[s_].wdeps(), ph6_done], chan=p7x_chan[s_]))
            yb = (t % 3) * 2
            wy = [bank[yb].wdeps(), bank[yb + 1].wdeps()]
            for nh in range(2):
                for ch in range(8):
                    last = mm(pb(yb + nh), catT[:, ch, t * 128:(t + 1) * 128], wout[:, ch, nh * 512:(nh + 1) * 512], ch == 0, ch == 7,
                              [ph5_done, ph6_done, ld_wout, wy[nh]])
                bank[yb + nh].wrote(last)
            st7[t] = [ld, last]
            return last

        def p7B1(t):
            s_ = t % 4
            c = t % 4
            q = t % 2
            yb = (t % 3) * 2
            ld, last = st7[t]
            a = act(p7junk[:], pb2(yb), AF.Square, [last, p7j_buf.wdeps()], accum=ssq2[:, c:c + 1])
            bank[yb].read(a); bank[yb + 1].read(a)
            p7j_buf.wrote(a)
            r = rstd_ops(rstd2[:, c:c + 1], ssq2[:, c:c + 1], D, [a, m_eps])
            t1 = stt("dve", p7t[q][:], pb2(yb), rstd2[:, c:c + 1], G1[:], ALU.mult, ALU.mult, [r, g1_ops, p7t_buf[q].wdeps()])
            bank[yb].read(t1); bank[yb + 1].read(t1)
            x1 = tt("dve", p7x[s_][:], p7x[s_][:], p7t[q][:], ALU.add, [t1, ld])
            p7t_buf[q].wrote(t1); p7t_buf[q].read(x1)
            st = dma("sp", x1_d[t * 128:(t + 1) * 128, :], p7x[s_][:], [x1], chan=st_chan[s_])
            x1_st.append(st)
            p7x_buf[s_].read(st)
            st7[t] = x1

        def p7B2(t):
            s_ = t % 4
            c = t % 4
            q = t % 2
            x1 = st7.pop(t)
            a2 = act(p7junk[:], p7x[s_][:], AF.Square, [x1, p7j_buf.wdeps()], accum=ssq[:, c:c + 1])
            p7j_buf.wrote(a2)
            r2 = rstd_ops(rstd[:, c:c + 1], ssq[:, c:c + 1], D, [a2, m_eps])
            m = ts("dve", p7xs2[q][:], p7x[s_][:], rstd[:, c:c + 1], None, ALU.mult, deps=[r2, p7xs_buf[q].wdeps()])
            p7xs_buf[q].wrote(m)
            p7x_buf[s_].read(m); p7x_buf[s_].read(a2)
            return m

        pair_w0 = {}

        tl7 = {}

        def p7C(t, m):
            q = t % 2
            j2 = t % 2
            if j2 == 0:
                pair_w0[0] = [bank[6].wdeps(), bank[7].wdeps()]
            for kc in range(8):
                tlast = tr(tp7v[0][:, kc, j2 * 128:(j2 + 1) * 128], p7xs2[q][:, kc * 128:(kc + 1) * 128], [m, pair_w0[0][kc // 4]])
            p7xs_buf[q].read(tlast)
            tl7[t] = tlast
            if j2 == 1:
                bank[6].wrote(tlast); bank[7].wrote(tlast)

        def p7E(t):
            tlast = tl7[t]
            act_last = None
            for kc in (0, 1, 4, 5, 6, 7, 2, 3):
                o_ap = h2T[:, kc, (t - 1) * 128:(t + 1) * 128]
                if kc < 2:
                    e = act(o_ap, tp7v[0][:, kc, :], AF.Identity, [tlast, A2op, ev_shf, ph5_done],
                            scale=Acol[:, 2, kc:kc + 1], bias=modc[:, 3, kc, 0:1])
                    act_last = e
                else:
                    e = ts("dve", o_ap, tp7v[0][:, kc, :], Acol[:, 2, kc:kc + 1], modc[:, 3, kc, 0:1], ALU.mult, ALU.add,
                           deps=[tlast, A2op, ev_shf, ph5_done, act_last if kc < 4 else None])
                bank[6 + kc // 4].read(e)
                h2_w.append(e)

        last = p7A(0)
        last = p7A(1)
        last = p7A(2)
        p7B1(0)
        p7B1(1)
        for t in range(16):
            if t + 3 < 16:
                last = p7A(t + 3)
            m_ = p7B2(t)
            p7C(t, m_)
            if t + 2 < 16:
                p7B1(t + 2)
            if t % 2 == 1:
                p7E(t)
        tap("h2T", h2T[:], h2_w)
        ph7_done = [h2_w, x1_st, last]

        if STOP == 8:
            raise _Stop
        sg_buf = [Buf(), Buf()]
        p8x_buf = [Buf(), Buf()]
        p8x_chan = [P.chan(), P.chan()]
        p8o_buf = [Buf(), Buf()]
        out_chan = [P.chan(), P.chan()]
        out_st = []
        cnt8 = {"sgi": 0, "gset": 0}
        blocks = [(half, fb) for half in range(2) for fb in range(6)]
        act_w = {0: [], 1: []}

        def compute_block(k):
            half, fb = blocks[k]
            tk0 = half * 1024
            ncol = 512 if fb < 5 else 256
            s_ = 2 if k == 0 else k % 2
            lg, lu = blk_ld[k]
            for fi in range(ncol // 128):
                fc = fb * 4 + fi
                for tq in range(2):
                    gb = (cnt8["gset"] % 3) * 2
                    cnt8["gset"] += 1
                    w0 = [bank[gb].wdeps(), bank[gb + 1].wdeps()]
                    for kc in range(8):
                        mg = mm(pb(gb), wst3[s_][0][:, kc, fi * 128:(fi + 1) * 128], h2T[:, kc, tk0 + tq * 512:tk0 + (tq + 1) * 512], kc == 0, kc == 7,
                                [lg, ph7_done, w0[0]])
                    bank[gb].wrote(mg)
                    for kc in range(8):
                        mu = mm(pb(gb + 1), wst3[s_][1][:, kc, fi * 128:(fi + 1) * 128], h2T[:, kc, tk0 + tq * 512:tk0 + (tq + 1) * 512], kc == 0, kc == 7,
                                [lu, w0[1]])
                    bank[gb + 1].wrote(mu)
                    wst_buf[s_].read(mu)
                    q = cnt8["sgi"] % 2
                    cnt8["sgi"] += 1
                    a = bank[gb].read(act(sg[q][:], pb(gb), AF.Silu, [mg, sg_buf[q].wdeps(), ph7_done]))
                    sg_buf[q].wrote(a)
                    o_ = tt("dve", actT[:, NFC - 1 - fc, tq * 512:(tq + 1) * 512], sg[q][:], pb(gb + 1), ALU.mult,
                            [a, mu, ph7_done, down_last])
                    bank[gb + 1].read(o_)
                    sg_buf[q].read(o_)
                    act_w[half].append(o_)

        def down_half(half):
            last = None
            for tl in range(8):
                tt_ = half * 8 + tl
                s = tt_ % 2
                ld = p8x_buf[s].wrote(dma("sp", p8x[s][:], x1_d[tt_ * 128:(tt_ + 1) * 128, :], [p8x_buf[s].wdeps(), x1_st, ph7_done], chan=p8x_chan[s]))
                zb = (tl % 2) * 2
                wz = [bank[zb].wdeps(), bank[zb + 1].wdeps()]
                for nh in range(2):
                    for fc in range(NFC):
                        last = mm(pb(zb + nh), actT[:, NFC - 1 - fc, tl * 128:(tl + 1) * 128], wdn[:, fc, nh * 512:(nh + 1) * 512], fc == 0, fc == NFC - 1,
                                  [act_w[half], ld_wd, wz[nh]])
                    bank[zb + nh].wrote(last)
                a = act(p8junk[:], pb2(zb), AF.Square, [last, p7j_buf.wdeps(), p8prev[s], h2_w], accum=ssq2[:, s:s + 1])
                bank[zb].read(a); bank[zb + 1].read(a)
                p7j_buf.wrote(a)
                r = rstd_ops(rstd2[:, s:s + 1], ssq2[:, s:s + 1], D, [a, m_eps])
                t1 = stt("dve", p8o[s][:], pb2(zb), rstd2[:, s:s + 1], G2[:], ALU.mult, ALU.mult, [r, g2_ops, p8o_buf[s].wdeps()])
                p8prev[s] = [r, t1]
                bank[zb].read(t1); bank[zb + 1].read(t1)
                o_ = tt("pool", p8o[s][:], p8o[s][:], p8x[s][:], ALU.add, [t1, ld])
                p8x_buf[s].read(o_)
                st = dma("sp", out_d[tt_ * 128:(tt_ + 1) * 128, :], p8o[s][:], [o_], chan=out_chan[s])
                p8o_buf[s].wrote(t1); p8o_buf[s].read(st)
                out_st.append(st)
            return last

        p8prev = [[], []]
        down_last = None
        issue_blk(1, [ph7_done])
        ld_wd = []
        wd_v = wd_d.rearrange("(fc p) n -> p fc n", p=128)
        wd_cut = [0, 6, 12, 17, NFC]
        for k in range(12):
            compute_block(k)
            if k + 2 < 12:
                issue_blk(k + 2, [ph7_done])
            if k < 4:
                ld_wd.append(dma("pool", wdn[:, wd_cut[k]:wd_cut[k + 1], :], wd_v[:, wd_cut[k]:wd_cut[k + 1], :], [ph5_done, ph6_done]))
            if k == 5:
                down_last = down_half(0)
        down_half(1)

    except _Stop:
        pass
    print('nchan', P.nchan, 'nops', len(P.ops))
    with ExitStack() as st:
        P.emit(st)
    return nc


def _consts():
    bf = ml_dtypes.bfloat16
    ident = np.eye(128, dtype=np.float32).astype(bf)
    cm = np.arange(128)
    angc = 2 * np.pi * np.outer(cm, cm) / 128.0
    csc = np.concatenate([np.cos(angc), np.sin(angc)], axis=1) / 512.0
    t = np.arange(S, dtype=np.int64)
    tk = (np.outer(t, t) % S).astype(np.float64) * (2 * np.pi / S)
    tab = np.where((t < S // 2)[:, None], np.cos(tk), np.sin(tk))
    tab[S // 2, :] = 0.0
    tab = tab.astype(np.float32).astype(bf)
    dft = np.ascontiguousarray(tab[:, :1024].reshape(16, 128, 1024))
    inv = (10000.0 ** (-np.arange(8, dtype=np.float32) / 8.0)).astype(np.float32)
    row = (t // 64).astype(np.float32)
    col = (t % 64).astype(np.float32)
    rope = np.zeros((2, 32, NK), dtype=np.float32)
    rope[0, :, :C] = 1.0
    for i in range(32):
        pos = row if i < 16 else col
        a = i % 16
        ang = (pos * inv[a % 8]).astype(np.float32)
        rope[0, i, C:] = np.cos(ang)
        rope[1, i, C:] = np.sin(ang) * (-1.0 if a < 8 else 1.0)
    return ident, csc.astype(np.float32).astype(bf), dft, rope


_SWAP = np.array([(i // 16) * 16 + ((i % 16) + 8) % 16 for i in range(32)])


def _alts():
    bf = ml_dtypes.bfloat16
    k = np.arange(1024)
    altk = np.where(k % 2 == 0, 1.0, -1.0).astype(np.float32).astype(bf).reshape(1, 1024)
    p = np.arange(128)
    altp = np.stack([np.where(p % 2 == 0, 1.0, -1.0), np.ones(128)], axis=1).astype(np.float32).astype(bf)
    return {"altk": altk, "altp": altp}


def _prep(inp, b):
    f = lambda a: np.ascontiguousarray(a, dtype=np.float32)
    w_in = inp["w_in"][0]
    w_in_x = np.concatenate([w_in, w_in[:, 896:928][:, _SWAP]], axis=1)
    wqb = inp["w_q_b"][0]
    nr = [wqb[:, h * 96 + 64:h * 96 + 96] for h in range(8)]
    sw = [wqb[:, h * 96 + 64:h * 96 + 96][:, _SWAP] for h in range(8)]
    wqb_x = np.concatenate([wqb] + nr + sw, axis=1)
    wkvb = inp["w_kv_b"][0].reshape(128, 8, 128)
    wkv = np.concatenate([wkvb[:, :, :64].reshape(128, 512), wkvb[:, :, 64:].reshape(128, 512)], axis=1)
    g4 = np.stack([inp["g_pre_mix"][0], inp["g_post_mix"][0], inp["g_pre_ffn"][0], inp["g_post_ffn"][0]])
    return {
        "x": f(inp["x"][b]), "ctx": f(inp["ctx"][b]),
        "cvec": f(np.stack([inp["c"][b], inp["c_ctx"]])),
        "cvec_c": f(np.stack([inp["c"][b], inp["c_ctx"]]).reshape(2, 8, 128).transpose(2, 1, 0)),
        "g4_c": f(g4.reshape(4, 8, 128).transpose(2, 0, 1)),
        "b_ada_c": f(inp["b_ada"][0].reshape(48, 128).T),
        "gq_c": f(inp["g_q_a"][0].reshape(2, 128).T),
        "gkv_c": f(inp["g_kv_a"][0].reshape(1, 128).T),
        "w_ada": f(inp["w_ada"][0]), "b_ada": f(inp["b_ada"][0]), "g4": f(g4),
        "w_in": f(w_in_x), "g_q_a": f(inp["g_q_a"][0]), "w_q_b": f(wqb_x),
        "g_kv_a": f(inp["g_kv_a"][0]), "w_kv": f(wkv),
        "w_f": f(inp["w_fourier"][0].reshape(512, 128)), "w_out": f(inp["w_out"][0]),
        "w_gate": f(inp["w_gate"][0]), "w_up": f(inp["w_up"][0]), "w_down": f(inp["w_down"][0]),
    }


def kernel(**inputs):
    inp = {k: np.asarray(v) for k, v in inputs.items()}
    ident, csc, dft, rope = _consts()
    rope4 = np.ascontiguousarray(np.tile(rope, (1, 4, 1)))
    nc = build()
    in_maps = []
    for b in range(8):
        m = _prep(inp, b)
        m.update(ident=ident, csc=csc, dft=dft, rope=rope4)
        m.update(_alts())
        in_maps.append(m)
    res = run_bass_kernel_spmd(nc, in_maps, core_ids=list(range(8)))
    return np.stack([np.asarray(r["out"], dtype=np.float32) for r in res.results], axis=0)
```
